# Optimizing a Trainium2 kernel written in Bass

```python
import jax, jax.numpy as jnp
from jax import lax
import numpy as np

D_MODEL = 1024
BATCH = 4
SEQ = 8192
DEPTH = 1

CHUNK = 64
MIX_WIDTH = D_MODEL
CONV_WIDTH = D_MODEL // 2
CONV_HEADS = 8
CONV_K = 31
GMLP_WIDTH = D_MODEL // 2
GMLP_HEADS = 8
GMLP_HEAD_DIM = GMLP_WIDTH // GMLP_HEADS
GMLP_CHUNK = 2 * CHUNK
IN_WIDTH = 2 * CONV_WIDTH + 2 * GMLP_WIDTH
D_FF = 2816
FFN_CONV_K = 3
NORM_EPS = 1e-6
LN_EPS = 1e-5

kernel_name = "hybrid_conformer_gmlp_convffn_block"


def rms_norm(x, g):
    xf = x.astype(jnp.float32)
    r = lax.rsqrt(jnp.mean(xf * xf, axis=-1, keepdims=True) + NORM_EPS)
    return (xf * r).astype(x.dtype) * g


def layer_norm(x, g, b):
    xf = x.astype(jnp.float32)
    mu = jnp.mean(xf, axis=-1, keepdims=True)
    var = jnp.mean(jnp.square(xf - mu), axis=-1, keepdims=True)
    return ((xf - mu) * lax.rsqrt(var + LN_EPS)).astype(x.dtype) * g + b


def causal_dwconv(x, w, b):
    k, c = w.shape
    y = lax.conv_general_dilated(
        x, w[:, None, :].astype(x.dtype), window_strides=(1,),
        padding=[(k - 1, 0)], dimension_numbers=("NWC", "WIO", "NWC"),
        feature_group_count=c)
    return y + b


def conformer_conv_group(a_val, a_gate, conv_w, conv_b, ln_g, ln_b):
    a = a_val * jax.nn.sigmoid(a_gate)
    a = causal_dwconv(a, conv_w, conv_b)
    a = layer_norm(a, ln_g, ln_b)
    return jax.nn.silu(a)


def gmlp_spatial_group(b_u, b_v, ln_g, ln_b, w_s, b_s):
    bsz, seq, _ = b_u.shape
    nb = seq // GMLP_CHUNK
    u = jax.nn.gelu(b_u, approximate=False)
    v = layer_norm(jax.nn.gelu(b_v, approximate=False), ln_g, ln_b)
    v = v.reshape(bsz, nb, GMLP_CHUNK, GMLP_HEADS, GMLP_HEAD_DIM)
    u = u.reshape(bsz, nb, GMLP_CHUNK, GMLP_HEADS, GMLP_HEAD_DIM)
    mask = jnp.tril(jnp.ones((GMLP_CHUNK, GMLP_CHUNK), dtype=w_s.dtype))
    w_c = w_s * mask[None]
    sp = jnp.einsum("hts,bnshd->bnthd", w_c, v)
    sp = sp + jnp.transpose(b_s)[None, None, :, :, None]
    return (u * sp).reshape(bsz, seq, GMLP_WIDTH)


def setup_inputs(seed: int = 0) -> dict:
    key = jax.random.key(seed)
    ks = jax.random.split(key, 24)
    f32 = jnp.float32
    nrm = lambda k, s, scale: jax.random.normal(k, s, f32) * scale
    L = DEPTH
    return {
        "x": jax.random.normal(ks[0], (BATCH, SEQ, D_MODEL), f32),
        "mix_norm_g": 1.0 + nrm(ks[1], (L, D_MODEL), 0.02),
        "w_in": nrm(ks[2], (L, D_MODEL, IN_WIDTH), D_MODEL ** -0.5),
        "b_in": nrm(ks[3], (L, IN_WIDTH), 0.02),
        "conv_a_w": nrm(ks[4], (L, CONV_K, CONV_WIDTH), CONV_K ** -0.5),
        "conv_a_b": nrm(ks[5], (L, CONV_WIDTH), 0.02),
        "ln_a_g": 1.0 + nrm(ks[6], (L, CONV_WIDTH), 0.02),
        "ln_a_b": nrm(ks[7], (L, CONV_WIDTH), 0.02),
        "ln_b_g": 1.0 + nrm(ks[8], (L, GMLP_WIDTH), 0.02),
        "ln_b_b": nrm(ks[9], (L, GMLP_WIDTH), 0.02),
        "w_spatial": nrm(ks[10], (L, GMLP_HEADS, GMLP_CHUNK, GMLP_CHUNK), GMLP_CHUNK ** -0.5),
        "b_spatial": 1.0 + nrm(ks[11], (L, GMLP_HEADS, GMLP_CHUNK), 0.02),
        "w_out": nrm(ks[12], (L, MIX_WIDTH, D_MODEL), MIX_WIDTH ** -0.5),
        "ffn_norm_g": 1.0 + nrm(ks[13], (L, D_MODEL), 0.02),
        "w_up": nrm(ks[14], (L, D_MODEL, 2 * D_FF), D_MODEL ** -0.5),
        "conv_f_w": nrm(ks[15], (L, FFN_CONV_K, 2 * D_FF), FFN_CONV_K ** -0.5),
        "conv_f_b": nrm(ks[16], (L, 2 * D_FF), 0.02),
        "w_down": nrm(ks[17], (L, D_FF, D_MODEL), D_FF ** -0.5),
        "final_norm_g": 1.0 + nrm(ks[18], (D_MODEL,), 0.02),
    }


def reference(x, mix_norm_g, w_in, b_in, conv_a_w, conv_a_b, ln_a_g, ln_a_b,
              ln_b_g, ln_b_b, w_spatial, b_spatial, w_out, ffn_norm_g, w_up,
              conv_f_w, conv_f_b, w_down, final_norm_g):
    h = x
    for l in range(DEPTH):
        y = rms_norm(h, mix_norm_g[l])
        z = jnp.einsum("bsd,de->bse", y, w_in[l]) + b_in[l]
        a_val = z[..., :CONV_WIDTH]
        a_gate = z[..., CONV_WIDTH:2 * CONV_WIDTH]
        b_u = z[..., 2 * CONV_WIDTH:2 * CONV_WIDTH + GMLP_WIDTH]
        b_v = z[..., 2 * CONV_WIDTH + GMLP_WIDTH:]
        out_a = conformer_conv_group(a_val, a_gate, conv_a_w[l], conv_a_b[l],
                                     ln_a_g[l], ln_a_b[l])
        out_b = gmlp_spatial_group(b_u, b_v, ln_b_g[l], ln_b_b[l],
                                   w_spatial[l], b_spatial[l])
        mixed = jnp.concatenate([out_a, out_b], axis=-1)
        h = h + jnp.einsum("bse,ed->bsd", mixed, w_out[l])
        y2 = rms_norm(h, ffn_norm_g[l])
        up = jnp.einsum("bsd,df->bsf", y2, w_up[l])
        up = causal_dwconv(up, conv_f_w[l], conv_f_b[l])
        gate, val = up[..., :D_FF], up[..., D_FF:]
        h = h + jnp.einsum("bsf,fd->bsd", jax.nn.silu(gate) * val, w_down[l])
    return rms_norm(h, final_norm_g)
```

```python
import numpy as np
from contextlib import ExitStack
import concourse.bass as bass
import concourse.mybir as mybir
from concourse.bass_utils import run_bass_kernel_spmd

F32 = mybir.dt.float32
BF16 = mybir.dt.bfloat16
AF = mybir.ActivationFunctionType
ALU = mybir.AluOpType

NCORES = 8
D = 1024
SEQ = 8192
TOK = 4096
HALO = 128
TT = 512
NT = TOK // TT
DFF = 2816
NFC = DFF // 128
KD = D // 128
RMS_EPS = 1e-6
LN_EPS = 1e-5
NSLOT = 4
NSET = 3
SLOT_EL = 4096
NUNIT = 23

C_G1, C_G2, C_G3 = 0, 8, 16
C_BIN = 24
C_CB = 36
C_LAG = 40
C_LAB = 44
C_CFW = 48
C_CFB = 180
C_HM = 224
C_ONE = 225
C_CAW = 226
C_EPSR = C_CAW + 124
C_EPSL = C_EPSR + 1
NV = C_EPSL + 1


class _OpRec:
    __slots__ = ("seq", "done", "clock")

    def __init__(self, seq, done, clock):
        self.seq = seq
        self.done = done
        self.clock = clock


class _Prog:
    ENG_SEM = {"pe": "s_pe", "act": "s_act", "dve": "s_dve", "pool": "s_pool"}

    def __init__(self):
        self.streams = {e: [] for e in ("pe", "act", "dve", "pool", "sp")}
        self.cnt = {}
        self.clock = {e: {} for e in self.streams}
        self.last_w = {}
        self.readers = {}
        self.seq = 0

    def op(self, eng, fn, reads=(), writes=(), sem=None, inc=1):
        deps = {}
        for k in reads:
            w = self.last_w.get(k)
            if w is not None:
                deps[w.seq] = w
        for k in writes:
            w = self.last_w.get(k)
            if w is not None:
                deps[w.seq] = w
            for r in self.readers.get(k, ()):
                deps[r.seq] = r
        clk = self.clock[eng]
        waits = []
        for s in sorted(deps, reverse=True):
            d = deps[s]
            sn, v = d.done
            if clk.get(sn, 0) >= v:
                continue
            waits.append((sn, v))
            clk[sn] = v
            for a, b in d.clock.items():
                if clk.get(a, 0) < b:
                    clk[a] = b
        semname = sem or self.ENG_SEM.get(eng)
        if inc:
            self.cnt[semname] = self.cnt.get(semname, 0) + inc
            done = (semname, self.cnt[semname])
        else:
            done = ("none", 0)
        self.seq += 1
        rec = _OpRec(self.seq, done, dict(clk))
        self.streams[eng].append((waits, fn, semname, inc))
        for k in writes:
            self.last_w[k] = rec
            self.readers[k] = []
        for k in reads:
            self.readers.setdefault(k, []).append(rec)
        return rec

    def emit(self, block, sems):
        def run(engname):
            def body(eng):
                for waits, fn, semname, inc in self.streams[engname]:
                    for sn, v in waits:
                        eng.wait_ge(sems[sn], v)
                    if fn is None:
                        continue
                    instrs = fn(eng)
                    if not inc:
                        continue
                    if not isinstance(instrs, (list, tuple)):
                        instrs = [instrs]
                    per = inc // len(instrs)
                    assert per * len(instrs) == inc
                    for ins in instrs:
                        ins.then_inc(sems[semname], per)
            return body

        block.tensor(run("pe"))
        block.scalar(run("act"))
        block.vector(run("dve"))
        block.gpsimd(run("pool"))
        block.sync(run("sp"))


def _ACT(**kw):
    return lambda e: e.activation(**kw)


def _TT(**kw):
    return lambda e: e.tensor_tensor(**kw)


def _STT(**kw):
    return lambda e: e.scalar_tensor_tensor(**kw)


def _TS(**kw):
    return lambda e: e.tensor_scalar(**kw)


def _CP(**kw):
    return lambda e: e.tensor_copy(**kw)


def _RCP(**kw):
    return lambda e: e.reciprocal(**kw)


def _MS(ap, val):
    return lambda e: e.memset(ap, val)


def _BNS(**kw):
    return lambda e: e.bn_stats(**kw)


def _BNA(**kw):
    return lambda e: e.bn_aggr(**kw)


def _MM(lst):
    def f(e):
        last = None
        for (o, l, r, st, sp) in lst:
            last = e.matmul(o, l, r, start=st, stop=sp)
        return last
    return f


def _DMA(pairs):
    return lambda e: [e.dma_start(out=o, in_=i) for o, i in pairs]


_DBG = {}


def build_nc(nt=NT):
    TOK = nt * TT
    NT_ = nt
    nc = bass.Bass("TRN2", target_bir_lowering=False)
    dt = nc.dram_tensor
    xT = dt("xT", [D, HALO + TOK], F32, kind="ExternalInput").ap()
    w_in = dt("w_in", [D, 2048], F32, kind="ExternalInput").ap()
    w_out = dt("w_out", [D, D], F32, kind="ExternalInput").ap()
    w_up = dt("w_up", [D, 2 * DFF], F32, kind="ExternalInput").ap()
    w_down = dt("w_down", [DFF, D], F32, kind="ExternalInput").ap()
    vecs_d = dt("vecs", [128, NV], F32, kind="ExternalInput").ap()
    tiles_d = dt("tiles", [128, 3, 512], F32, kind="ExternalInput").ap()
    wsT_d = dt("wsT", [128, 8, 128], F32, kind="ExternalInput").ap()
    mask_d = dt("maskT", [128, 128], F32, kind="ExternalInput").ap()
    bs_d = dt("bs", [128, 4, 128], F32, kind="ExternalInput").ap()
    ident_d = dt("ident", [128, 128], F32, kind="ExternalInput").ap()
    outT = dt("outT", [D, TOK], F32, kind="ExternalOutput").ap()
    scr = dt("scr", [NUNIT, 128, SLOT_EL], BF16, kind="Internal").ap()

    xT_v = xT.rearrange("(k p) t -> p k t", p=128)
    outT_v = outT.rearrange("(k p) t -> p k t", p=128)
    w_in_v = w_in.rearrange("(k p) e -> p k e", p=128)
    w_out_v = w_out.rearrange("(k p) e -> p k e", p=128)
    w_up_v = w_up.rearrange("(k p) e -> p k e", p=128)
    w_dn_v = w_down.rearrange("(c p) e -> p c e", p=128)

    P = _Prog()
    semnames = ["s_pe", "s_act", "s_dve", "s_pool", "cst", "xl0", "xl1", "os0", "os1"]
    semnames += [f"rl{i}" for i in range(NSLOT)] + [f"rs{i}" for i in range(NSLOT)] + [f"rc{i}" for i in range(NSLOT)]

    with ExitStack() as es:
        sems = {n: es.enter_context(nc.semaphore(n)) for n in semnames}
        sb = lambda name, shape, dtp: es.enter_context(nc.sbuf_tensor(name, shape, dtp))
        ring = [sb(f"ring{i}", [128, SLOT_EL], BF16) for i in range(NSLOT)]
        xbuf = [sb(f"xb{i}", [128, KD, TT], F32) for i in range(2)]
        y = sb("y", [128, KD, TT], BF16)
        y1 = sb("y1", [128, KD, TT], BF16)
        mx = sb("mx", [128, KD, TT], BF16)
        a = sb("a", [128, 4, 30 + TT], BF16)
        cu = sb("cu", [128, 4, TT], F32)
        vt = sb("vt", [128, 4, 512], F32)
        vlnz = sb("vlnz", [128, 4, 4, 2, 128], BF16)
        wsT = vt[:, 0:2, :].rearrange("p a (b c) -> p (a b) c", c=128)
        st_r = sb("st_r", [128, TT], F32)
        st_mu = sb("st_mu", [128, TT], F32)
        st_t = sb("st_t", [128, TT], F32)
        st_ra = sb("st_ra", [128, TT], F32)
        g = sb("g", [128, NFC, TT], BF16)
        acc = [[sb(f"acc{i}{j}", [128, TT], F32) for j in range(2)] for i in range(NSET)]
        uh = [sb(f"uh{i}", [128, NFC, 2, 2], F32) for i in range(2)]
        bbuf = [sb(f"bb{i}", [128, TT], F32) for i in range(2)]
        corr = sb("corr", [128, NFC, 2, 2], F32)
        ctmp = sb("ctmp", [128, NFC, 2], F32)
        yh = sb("yh", [128, KD, 2], BF16)
        diag = sb("diag", [128, 124, 128], BF16)
        maskT = sb("mask_s", [128, 128], F32)
        wsm = sb("wsm", [128, 8, 128], BF16)
        bs = sb("bs_s", [128, 4, 128], F32)
        tiles = sb("tiles_s", [128, 3, 512], F32)
        ident = sb("ident_s", [128, 128], F32)
        vecs = sb("vecs_s", [128, NV], F32)
        onesm = sb("onesm", [128, 128], BF16)
        ones5 = sb("ones5", [128, 128], BF16)
        bst = sb("bst", [128, 4, 6], F32)
        bag = sb("bag", [128, 4, 2], F32)
        rsb = sb("rsb", [128, 4], F32)
        psb = [es.enter_context(nc.psum_tensor(f"ps{i}", [128, 512], F32)) for i in range(8)]
        block = es.enter_context(nc.Block())

        state = {"bank": 0, "unit_seq": 0, "held": set()}

        def next_bank(hold=False):
            for _ in range(8):
                b = state["bank"]
                state["bank"] = (b + 1) % 8
                if b not in state["held"]:
                    if hold:
                        state["held"].add(b)
                    return b
            raise RuntimeError("all PSUM banks held")

        def release(*banks):
            for b in banks:
                state["held"].discard(b)

        PS = lambda b: ("ps", b)
        vcol = lambda c: vecs[:, c:c + 1]

        P.op("sp", _DMA([(vecs[:], vecs_d), (tiles[:], tiles_d), (wsT, wsT_d), (maskT[:], mask_d),
                         (bs[:], bs_d), (ident[:], ident_d)]),
             writes=["vecs", "tiles", ("vt", 0), ("vt", 1), "maskT", "bs", "ident"], sem="cst", inc=16 * 6)
        P.op("pool", _MS(vlnz[:], 0.0), writes=["vlnz"])
        P.op("pool", _MS(a[:], 0.0), writes=[("a", c) for c in range(4)])
        P.op("dve", _MS(onesm[:], 1.0 / D), writes=["onesm"])
        P.op("dve", _MS(ones5[:], 1.0 / 512), writes=["ones5"])
        for h in range(8):
            P.op("dve", _TT(out=wsm[:, h, :], in0=wsT[:, h, :], in1=maskT[:], op=ALU.mult),
                 reads=[("vt", 0), ("vt", 1), "maskT"], writes=[("wsm", h)])

        def build_diag():
            for i in range(124):
                if i % 2 == 0:
                    P.op("dve", _TS(out=diag[:, i, :], in0=ident[:], scalar1=vcol(C_CAW + i), scalar2=None, op0=ALU.mult),
                         reads=["ident", "vecs"], writes=[("diag", i)])
                else:
                    P.op("act", _ACT(out=diag[:, i, :], in_=ident[:], func=AF.Identity, bias=0.0, scale=vcol(C_CAW + i)),
                         reads=["ident", "vecs"], writes=[("diag", i)])

        in_cols = {0: 512, 1: 0, 2: 1024, 3: 1536}
        dn_groups = [(0, 8), (8, 8), (16, 6)]
        unit_in_scr = set()
        unit_slot = {}
        pending = []

        def unit_src_dmas(u, slot):
            r = ring[slot]
            rv = r[:].rearrange("p (k e) -> p k e", k=KD)
            if u < 4:
                c0 = in_cols[u]
                return [(rv, w_in_v[:, :, c0:c0 + 512])]
            if u < 6:
                c0 = (u - 4) * 512
                return [(rv, w_out_v[:, :, c0:c0 + 512])]
            if u < 17:
                j = u - 6
                res = []
                for pr in range(2):
                    jj = 2 * j + pr
                    res.append((rv[:, :, (2 * pr) * 128:(2 * pr + 1) * 128], w_up_v[:, :, jj * 128:(jj + 1) * 128]))
                    res.append((rv[:, :, (2 * pr + 1) * 128:(2 * pr + 2) * 128],
                                w_up_v[:, :, DFF + jj * 128:DFF + (jj + 1) * 128]))
                return res
            q = u - 17
            sw, gi = q // 3, q % 3
            f0, n = dn_groups[gi]
            return [(r[:, 0:n * 512].rearrange("p (c e) -> p c e", c=n), w_dn_v[:, f0:f0 + n, sw * 512:(sw + 1) * 512])]

        def issue_load(u):
            slot = state["unit_seq"] % NSLOT
            state["unit_seq"] += 1
            unit_slot[u] = slot
            if _DBG.get('noscr') or u not in unit_in_scr:
                pairs = unit_src_dmas(u, slot)
                P.op("pool", _DMA(pairs), writes=[("ring", slot)], sem=f"rc{slot}", inc=16 * len(pairs))
                if not _DBG.get('noscr'):
                    P.op("sp", _DMA([(scr[u], ring[slot][:])]), reads=[("ring", slot)], writes=[("scr", u)],
                         sem=f"rs{slot}", inc=16)
                unit_in_scr.add(u)
            else:
                P.op("sp", _DMA([(ring[slot][:], scr[u])]), reads=[("scr", u)], writes=[("ring", slot)],
                     sem=f"rl{slot}", inc=16)

        def prefetch():
            for ent in pending[:NSLOT - 1]:
                if not ent[1]:
                    issue_load(ent[0])
                    ent[1] = True

        def need(u):
            assert pending and pending[0][0] == u, (u, pending[:3])
            prefetch()
            pending.pop(0)
            prefetch()

        def load_x(ti, c0, T):
            s = ti % 2
            P.op("sp", _DMA([(xbuf[s][:, :, 0:T], xT_v[:, :, c0:c0 + T])]),
                 writes=[("x", s, k) for k in range(KD)], sem=f"xl{s}", inc=16)

        def slotv(slot):
            return ring[slot][:].rearrange("p (k e) -> p k e", k=KD)

        def rms_stages(ti, T, gcol, mode):
            s = ti % 2
            xs = xbuf[s]
            st = {}

            def s_sq():
                for k in range(KD):
                    P.op("act", _ACT(out=mx[:, k, 0:T], in_=xs[:, k, 0:T], func=AF.Square),
                         reads=[("x", s, k)], writes=[("mx", k)])

            def s_stat():
                b = next_bank(hold=True)
                st["b"] = b
                P.op("pe", _MM([(psb[b][:, 0:T], onesm[:], mx[:, k, 0:T], k == 0, k == KD - 1) for k in range(KD)]),
                     reads=[("mx", k) for k in range(KD)] + ["onesm"], writes=[PS(b)])

            def s_y():
                b = st["b"]
                P.op("act", _ACT(out=st_r[:, 0:T], in_=psb[b][:, 0:T], func=AF.Ln, bias=vcol(C_EPSR), scale=1.0),
                     reads=[PS(b), "vecs"], writes=["st_r"])
                release(b)
                P.op("act", _ACT(out=st_r[:, 0:T], in_=st_r[:, 0:T], func=AF.Exp, scale=-0.5), reads=["st_r"], writes=["st_r"])
                for k in range(KD):
                    if mode == "final":
                        dst, key = xs[:, k, 0:T], ("x", s, k)
                    elif mode == "y1":
                        dst, key = y1[:, k, 0:T], ("y1", k)
                    else:
                        dst, key = y[:, k, 0:T], ("y", k)
                    P.op("dve", _STT(out=dst, in0=xs[:, k, 0:T], scalar=vcol(gcol + k), in1=st_r[:, 0:T],
                                     op0=ALU.mult, op1=ALU.mult),
                         reads=[("x", s, k), "st_r", "vecs"], writes=[key])
            return [(s_sq, None), (s_stat, None), (s_y, None)]

        def mixer_stages(ti, T, is_halo):
            s = ti % 2
            xs = xbuf[s]
            nsub = T // 128
            yk = [("y1", k) for k in range(KD)]
            st = {}

            def s_in(u):
                def f():
                    slot = unit_slot[u]
                    sv = slotv(slot)
                    for ecl in range(4):
                        b = next_bank()
                        P.op("pe", _MM([(psb[b][:, 0:T], sv[:, k, ecl * 128:(ecl + 1) * 128], y1[:, k, 0:T], k == 0, k == KD - 1)
                                        for k in range(KD)]),
                             reads=yk + [("ring", slot)], writes=[PS(b)])
                        if u == 0:
                            P.op("act", _ACT(out=cu[:, ecl, 0:T], in_=psb[b][:, 0:T], func=AF.Sigmoid,
                                             bias=vcol(C_BIN + 4 + ecl), scale=1.0),
                                 reads=[PS(b), "vecs"], writes=[("cu", ecl)])
                        elif u == 1:
                            P.op("dve", _STT(out=a[:, ecl, 30:30 + T], in0=psb[b][:, 0:T], scalar=vcol(C_BIN + ecl),
                                             in1=cu[:, ecl, 0:T], op0=ALU.add, op1=ALU.mult),
                                 reads=[PS(b), "vecs", ("cu", ecl)], writes=[("a", ecl)])
                        else:
                            P.op("act", _ACT(out=cu[:, ecl, 0:T], in_=psb[b][:, 0:T], func=AF.Gelu,
                                             bias=vcol(C_BIN + 8 + ecl), scale=1.0),
                                 reads=[PS(b), "vecs"], writes=[("cu", ecl)])
                return f

            def s_invv():
                slot = unit_slot[3]
                sv = slotv(slot)
                for sidx in range(nsub):
                    b = next_bank()
                    P.op("pe", _MM([(psb[b][:, 0:512], y1[:, k, sidx * 128:(sidx + 1) * 128], sv[:, k, 0:512], k == 0, k == KD - 1)
                                    for k in range(KD)]),
                         reads=yk + [("ring", slot)], writes=[PS(b)])
                    P.op("dve", _TT(out=vt[:, sidx, :], in0=psb[b][:, 0:512], in1=tiles[:, 0, :], op=ALU.add),
                         reads=[PS(b), "tiles"], writes=[("vt", sidx)])
                    P.op("act", _ACT(out=vt[:, sidx, :], in_=vt[:, sidx, :], func=AF.Gelu),
                         reads=[("vt", sidx)], writes=[("vt", sidx)])
                    P.op("dve", _BNS(out=bst[:, sidx, :], in_=vt[:, sidx, :]), reads=[("vt", sidx)], writes=[("bst", sidx)])
                    P.op("dve", _BNA(out=bag[:, sidx, :], in_=bst[:, sidx, :]), reads=[("bst", sidx)], writes=[("bag", sidx)])

            def s_conv(ch):
                def f():
                    b = next_bank()
                    P.op("pe", _MM([(psb[b][:, 0:T], diag[:, ch * 31 + tap, :], a[:, ch, tap:tap + T], tap == 0, tap == 30)
                                    for tap in range(31)]),
                         reads=[("a", ch)] + [("diag", ch * 31 + tap) for tap in range(31)], writes=[PS(b)])
                    P.op("act", _ACT(out=cu[:, ch, 0:T], in_=psb[b][:, 0:T], func=AF.Identity, bias=vcol(C_CB + ch), scale=1.0),
                         reads=[PS(b), "vecs"], writes=[("cu", ch)])
                    P.op("act", _ACT(out=y1[:, ch, 0:T], in_=psb[b][:, 0:T], func=AF.Identity, bias=vcol(C_CB + ch), scale=1.0),
                         reads=[PS(b), "vecs"], writes=[("y1", ch)])
                    P.op("act", _ACT(out=y1[:, 4 + ch, 0:T], in_=psb[b][:, 0:T], func=AF.Square, bias=vcol(C_CB + ch), scale=1.0),
                         reads=[PS(b), "vecs"], writes=[("y1", 4 + ch)])
                return f

            def s_vln():
                P.op("act", _ACT(out=rsb[:, 0:nsub], in_=bag[:, 0:nsub, 1], func=AF.Ln, bias=vcol(C_EPSL), scale=1.0),
                     reads=[("bag", i) for i in range(nsub)] + ["vecs"], writes=["rsb"])
                P.op("act", _ACT(out=rsb[:, 0:nsub], in_=rsb[:, 0:nsub], func=AF.Exp, scale=-0.5), reads=["rsb"], writes=["rsb"])
                for sidx in range(nsub):
                    P.op("dve", _TS(out=vt[:, sidx, :], in0=vt[:, sidx, :], scalar1=bag[:, sidx, 0:1],
                                    scalar2=rsb[:, sidx:sidx + 1], op0=ALU.subtract, op1=ALU.mult),
                         reads=[("vt", sidx), ("bag", sidx), "rsb"], writes=[("vt", sidx)])
                    P.op("dve", _TT(out=vt[:, sidx, :], in0=vt[:, sidx, :], in1=tiles[:, 1, :], op=ALU.mult),
                         reads=[("vt", sidx), "tiles"], writes=[("vt", sidx)])
                    for hh in range(2):
                        P.op("dve", _TT(out=vlnz[:, sidx, :, hh, hh * 64:(hh + 1) * 64],
                                        in0=vt[:, sidx, :].rearrange("p (c q) -> p c q", c=4)[:, :, hh * 64:(hh + 1) * 64],
                                        in1=tiles[:, 2, :].rearrange("p (c q) -> p c q", c=4)[:, :, hh * 64:(hh + 1) * 64],
                                        op=ALU.add),
                             reads=[("vt", sidx), "tiles", "vlnz"], writes=[("vlnz", sidx, hh)])

            def s_sp():
                for ch in range(4):
                    b = next_bank()
                    lst = []
                    for sidx in range(nsub):
                        lst.append((psb[b][:, sidx * 128:(sidx + 1) * 128], vlnz[:, sidx, ch, 0, :], wsm[:, 2 * ch, :], True, False))
                        lst.append((psb[b][:, sidx * 128:(sidx + 1) * 128], vlnz[:, sidx, ch, 1, :], wsm[:, 2 * ch + 1, :], False, True))
                    P.op("pe", _MM(lst),
                         reads=[("vlnz", i, hh) for i in range(nsub) for hh in range(2)] + [("wsm", 2 * ch), ("wsm", 2 * ch + 1)],
                         writes=[PS(b)])
                    for sidx in range(nsub):
                        P.op("dve", _TT(out=st_t[:, sidx * 128:(sidx + 1) * 128], in0=psb[b][:, sidx * 128:(sidx + 1) * 128],
                                        in1=bs[:, ch, :], op=ALU.add),
                             reads=[PS(b), "bs"], writes=["st_t"])
                    P.op("dve", _TT(out=mx[:, 4 + ch, 0:T], in0=st_t[:, 0:T], in1=cu[:, ch, 0:T], op=ALU.mult),
                         reads=["st_t", ("cu", ch)], writes=[("mx", 4 + ch)])

            def s_cev():
                mcol = C_HM if is_halo else C_ONE
                P.op("dve", _TS(out=a[:, :, 0:30], in0=a[:, :, T:T + 30], scalar1=vcol(mcol), scalar2=None, op0=ALU.mult),
                     reads=[("a", c) for c in range(4)] + ["vecs"], writes=[("a", c) for c in range(4)])

            def s_st2():
                bm = next_bank(hold=True)
                be = next_bank(hold=True)
                st["bm"], st["be"] = bm, be
                lst = [(psb[bm][:, 0:T], ones5[:], y1[:, ch, 0:T], ch == 0, ch == 3) for ch in range(4)]
                lst += [(psb[be][:, 0:T], ones5[:], y1[:, 4 + ch, 0:T], ch == 0, ch == 3) for ch in range(4)]
                P.op("pe", _MM(lst), reads=yk + ["ones5"], writes=[PS(bm), PS(be)])

            def s_lna():
                bm, be = st["bm"], st["be"]
                P.op("dve", _CP(out=st_mu[:, 0:T], in_=psb[bm][:, 0:T]), reads=[PS(bm)], writes=["st_mu"])
                P.op("dve", _TT(out=st_t[:, 0:T], in0=st_mu[:, 0:T], in1=st_mu[:, 0:T], op=ALU.mult),
                     reads=["st_mu"], writes=["st_t"])
                P.op("dve", _TT(out=st_t[:, 0:T], in0=psb[be][:, 0:T], in1=st_t[:, 0:T], op=ALU.subtract),
                     reads=[PS(be), "st_t"], writes=["st_t"])
                release(bm, be)
                P.op("act", _ACT(out=st_ra[:, 0:T], in_=st_t[:, 0:T], func=AF.Ln, bias=vcol(C_EPSL), scale=1.0),
                     reads=["st_t", "vecs"], writes=["st_ra"])
                P.op("act", _ACT(out=st_ra[:, 0:T], in_=st_ra[:, 0:T], func=AF.Exp, scale=-0.5), reads=["st_ra"], writes=["st_ra"])
                for ch in range(4):
                    P.op("dve", _TT(out=cu[:, ch, 0:T], in0=cu[:, ch, 0:T], in1=st_mu[:, 0:T], op=ALU.subtract),
                         reads=[("cu", ch), "st_mu"], writes=[("cu", ch)])
                    P.op("dve", _TT(out=cu[:, ch, 0:T], in0=cu[:, ch, 0:T], in1=st_ra[:, 0:T], op=ALU.mult),
                         reads=[("cu", ch), "st_ra"], writes=[("cu", ch)])
                    P.op("act", _ACT(out=mx[:, ch, 0:T], in_=cu[:, ch, 0:T], func=AF.Silu,
                                     bias=vcol(C_LAB + ch), scale=vcol(C_LAG + ch)),
                         reads=[("cu", ch), "vecs"], writes=[("mx", ch)])

            def s_out(u):
                def f():
                    mxk = [("mx", k) for k in range(KD)]
                    slot = unit_slot[u]
                    sv = slotv(slot)
                    for dcl in range(4):
                        dc = (u - 4) * 4 + dcl
                        b = next_bank()
                        P.op("pe", _MM([(psb[b][:, 0:T], sv[:, k, dcl * 128:(dcl + 1) * 128], mx[:, k, 0:T], k == 0, k == KD - 1)
                                        for k in range(KD)]),
                             reads=mxk + [("ring", slot)], writes=[PS(b)])
                        P.op("dve", _TT(out=xs[:, dc, 0:T], in0=xs[:, dc, 0:T], in1=psb[b][:, 0:T], op=ALU.add),
                             reads=[PS(b), ("x", s, dc)], writes=[("x", s, dc)])
                return f

            return [(s_in(0), 0), (s_in(1), 1), (s_in(2), 2), (s_invv, 3), (s_vln, None), (s_sp, None),
                    (s_conv(0), None), (s_conv(1), None), (s_conv(2), None), (s_conv(3), None),
                    (s_cev, None), (s_st2, None), (s_lna, None), (s_out(4), 4), (s_out(5), 5)]

        def ffn_slots(ti, T, first, last_tile):
            s = ti % 2
            xs = xbuf[s]
            cur, nxt = ti % 2, (ti + 1) % 2
            yk = [("y", k) for k in range(KD)]
            res = []
            pend = []

            def s_pair(j, pr):
                def f():
                    slot = unit_slot[6 + j]
                    sv = slotv(slot)
                    jj = 2 * j + pr
                    bg, bv = next_bank(), next_bank()
                    wg = lambda k: sv[:, k, (2 * pr) * 128:(2 * pr + 1) * 128]
                    wv = lambda k: sv[:, k, (2 * pr + 1) * 128:(2 * pr + 2) * 128]
                    lst = [(psb[bg][:, 0:T], wg(k), y[:, k, 0:T], k == 0, k == KD - 1) for k in range(KD)]
                    lst += [(psb[bv][:, 0:T], wv(k), y[:, k, 0:T], k == 0, k == KD - 1) for k in range(KD)]
                    P.op("pe", _MM(lst), reads=yk + [("ring", slot)], writes=[PS(bg), PS(bv)])
                    if first:
                        bh = next_bank()
                        lst = [(psb[bh][:, 0:2], wg(k), yh[:, k, :], k == 0, k == KD - 1) for k in range(KD)]
                        lst += [(psb[bh][:, 2:4], wv(k), yh[:, k, :], k == 0, k == KD - 1) for k in range(KD)]
                        P.op("pe", _MM(lst), reads=["yh", ("ring", slot)], writes=[PS(bh)])
                        P.op("act", _ACT(out=uh[cur][:, jj, :, :], in_=psb[bh][:, 0:4].rearrange("p (a b) -> p a b", a=2),
                                         func=AF.Identity, bias=0.0, scale=vcol(C_HM)),
                             reads=[PS(bh), "vecs"], writes=[("uh", cur, jj)])
                    ag, av = acc[jj % NSET]
                    bbg = bbuf[jj % 2]
                    AK = ("acc", jj % NSET)
                    BK = ("bb", jj % 2)
                    cw = lambda tap, v: vcol(C_CFW + tap * 44 + v * NFC + jj)
                    cb = lambda v: vcol(C_CFB + v * NFC + jj)
                    P.op("act", _ACT(out=ag[:, 0:T], in_=psb[bg][:, 0:T], func=AF.Identity, bias=cb(0), scale=cw(2, 0)),
                         reads=[PS(bg), "vecs"], writes=[AK + (0,)])
                    P.op("act", _ACT(out=bbg[:, 0:T], in_=psb[bg][:, 0:T], func=AF.Identity, bias=0.0, scale=cw(1, 0)),
                         reads=[PS(bg), "vecs"], writes=[BK])
                    P.op("act", _ACT(out=av[:, 0:T], in_=psb[bv][:, 0:T], func=AF.Identity, bias=cb(1), scale=cw(2, 1)),
                         reads=[PS(bv), "vecs"], writes=[AK + (1,)])
                    for v, (ab, pb) in enumerate(((ag, bg), (av, bv))):
                        K = AK + (v,)
                        after_act = [K, BK] if v == 0 else [K]
                        if not last_tile:
                            P.op("dve", _CP(out=uh[nxt][:, jj, v, :], in_=psb[pb][:, T - 2:T]),
                                 reads=[PS(pb)] + after_act,
                                 writes=[("uh", nxt, jj, v)] + ([("uh", nxt, jj)] if ti == 2 else []))
                        P.op("dve", _STT(out=ab[:, 2:T], in0=psb[pb][:, 0:T - 2], scalar=cw(0, v), in1=ab[:, 2:T],
                                         op0=ALU.mult, op1=ALU.add),
                             reads=[PS(pb), "vecs"] + after_act, writes=[K])
                        if v == 1:
                            P.op("dve", _STT(out=ab[:, 1:T], in0=psb[pb][:, 0:T - 1], scalar=cw(1, v), in1=ab[:, 1:T],
                                             op0=ALU.mult, op1=ALU.add),
                                 reads=[PS(pb), "vecs", K], writes=[K])
                        if first:
                            P.op("dve", _STT(out=ab[:, 0:2], in0=uh[cur][:, jj, v, :], scalar=cw(0, v), in1=ab[:, 0:2],
                                             op0=ALU.mult, op1=ALU.add),
                                 reads=[("uh", cur, jj), "vecs", K], writes=[K])
                            P.op("dve", _STT(out=ab[:, 0:1], in0=uh[cur][:, jj, v, 1:2], scalar=cw(1, v), in1=ab[:, 0:1],
                                             op0=ALU.mult, op1=ALU.add),
                                 reads=[("uh", cur, jj), "vecs", K], writes=[K])
                        else:
                            P.op("dve", _TT(out=ab[:, 0:2], in0=ab[:, 0:2], in1=corr[:, jj, v, :], op=ALU.add),
                                 reads=["corr", K], writes=[K])
                        if v == 0:
                            P.op("pool", _TT(out=ab[:, 1:T], in0=ab[:, 1:T], in1=bbg[:, 0:T - 1], op=ALU.add),
                                 reads=[K, BK], writes=[K])

                    def tail(ag=ag, av=av, jj=jj, AK=AK):
                        P.op("act", _ACT(out=ag[:, 0:T], in_=ag[:, 0:T], func=AF.Silu), reads=[AK + (0,)], writes=[AK + (0,)])
                        P.op("pool", _TT(out=g[:, jj, 0:T], in0=ag[:, 0:T], in1=av[:, 0:T], op=ALU.mult),
                             reads=[AK + (0,), AK + (1,)], writes=[("g", jj)])
                    if pend:
                        pend.pop(0)()
                    pend.append(tail)
                    if jj == NFC - 1:
                        pend.pop(0)()
                return f

            def s_corr():
                w0v = vecs[:, C_CFW:C_CFW + 44].rearrange("p (v j) -> p j v", v=2)
                w1v = vecs[:, C_CFW + 44:C_CFW + 88].rearrange("p (v j) -> p j v", v=2)
                uk = [("uh", cur, jj, v) for jj in range(NFC) for v in range(2)]
                P.op("dve", _TT(out=corr[:, :, :, 0], in0=uh[cur][:, :, :, 0], in1=w0v, op=ALU.mult),
                     reads=uk + ["vecs"], writes=["corr"])
                P.op("dve", _TT(out=ctmp[:], in0=uh[cur][:, :, :, 1], in1=w1v, op=ALU.mult),
                     reads=uk + ["vecs"], writes=["ctmp"])
                P.op("dve", _TT(out=corr[:, :, :, 0], in0=corr[:, :, :, 0], in1=ctmp[:], op=ALU.add),
                     reads=["corr", "ctmp"], writes=["corr"])
                P.op("dve", _TT(out=corr[:, :, :, 1], in0=uh[cur][:, :, :, 1], in1=w0v, op=ALU.mult),
                     reads=uk + ["vecs", "corr"], writes=["corr"])

            def s_unit(j):
                f0, f1 = s_pair(j, 0), s_pair(j, 1)

                def f():
                    if j == 0 and not first:
                        s_corr()
                    f0()
                    f1()
                return f

            for j in range(11):
                res.append((s_unit(j), 6 + j))

            dstate = {}

            def s_down(sw, gi):
                def f():
                    if gi == 0:
                        dstate["banks"] = [next_bank(hold=True) for _ in range(4)]
                    banks = dstate["banks"]
                    f0, n = dn_groups[gi]
                    slot = unit_slot[17 + sw * 3 + gi]
                    rv = ring[slot][:, 0:n * 512].rearrange("p (c e) -> p c e", c=n)
                    lst = []
                    for i in range(4):
                        for c in range(n):
                            fc = f0 + c
                            lst.append((psb[banks[i]][:, 0:T], rv[:, c, i * 128:(i + 1) * 128], g[:, fc, 0:T],
                                        fc == 0, fc == NFC - 1))
                    wr = [PS(b) for b in banks] if gi in (0, 2) else []
                    P.op("pe", _MM(lst), reads=[("g", f0 + c) for c in range(n)] + [("ring", slot)], writes=wr)
                    if gi == 2:
                        for i in range(4):
                            dc = sw * 4 + i
                            P.op("dve", _TT(out=xs[:, dc, 0:T], in0=xs[:, dc, 0:T], in1=psb[banks[i]][:, 0:T], op=ALU.add),
                                 reads=[PS(banks[i]), ("x", s, dc)], writes=[("x", s, dc)])
                        release(*banks)
                return f

            for sw in range(2):
                for gi in range(3):
                    res.append((s_down(sw, gi), 17 + sw * 3 + gi))
            return res

        def store_out(ti, c0, T):
            def f():
                s = ti % 2
                oc = c0 - HALO
                P.op("sp", _DMA([(outT_v[:, :, oc:oc + T], xbuf[s][:, :, 0:T])]),
                     reads=[("x", s, k) for k in range(KD)], writes=[("out", ti)], sem=f"os{s}", inc=16)
            return f

        tiles_sched = [(0, 0, HALO, True)] + [(1 + i, HALO + i * TT, TT, False) for i in range(NT_)]
        plan = []

        def lx(ti):
            t = tiles_sched[ti]
            return (lambda: load_x(t[0], t[1], t[2]), None)

        plan.append(lx(0))
        plan.append(lx(1))
        for ti in (0, 1):
            _, c0, T, is_halo = tiles_sched[ti]
            plan += rms_stages(ti, T, C_G1, "y1")
            ms = mixer_stages(ti, T, is_halo)
            if ti == 0:
                ms.insert(1, (build_diag, None))
            plan += ms
            plan += rms_stages(ti, T, C_G2, "y2")
            if is_halo:
                plan.append((lambda T=T: P.op("dve", _CP(out=yh[:], in_=y[:, :, T - 2:T]),
                                              reads=[("y", k) for k in range(KD)], writes=["yh"]), None))
                if NT_ >= 2:
                    plan.append(lx(2))
        SLOT_MAP = _DBG.get("slot_map") or [5, 6, 6,
                                            7, 7, 8, 8, 8, 10,
                                            10, 10, 11, 11,
                                            11, 12, 12, 13, 13,
                                            14, 14, 15]
        for ti in range(1, NT_ + 1):
            _, c0, T, _ = tiles_sched[ti]
            slots = ffn_slots(ti, T, first=(ti == 1), last_tile=(ti == NT_))
            inter = {}
            if ti > 1:
                _, pc0, pT, _ = tiles_sched[ti - 1]
                r3 = rms_stages(ti - 1, pT, C_G3, "final")
                inter.setdefault(0, []).append(r3[0])
                inter.setdefault(1, []).append(r3[1])
                inter.setdefault(1, []).append(r3[2])
                inter[1].append((store_out(ti - 1, pc0, pT), None))
                if ti + 1 <= NT_:
                    inter[1].append(lx(ti + 1))
            if ti + 1 <= NT_:
                _, nc0, nT, _ = tiles_sched[ti + 1]
                stg = rms_stages(ti + 1, nT, C_G1, "y1") + mixer_stages(ti + 1, nT, False) + rms_stages(ti + 1, nT, C_G2, "y2")
                assert len(stg) == len(SLOT_MAP)
                for st_, sl in zip(stg, SLOT_MAP):
                    inter.setdefault(sl, []).append(st_)
            for si, ent in enumerate(slots):
                plan.append(ent)
                plan.extend(inter.get(si, []))
        _, lc0, lT, _ = tiles_sched[NT_]
        plan += rms_stages(NT_, lT, C_G3, "final")
        plan.append((store_out(NT_, lc0, lT), None))

        for fn, u in plan:
            if u is not None:
                pending.append([u, False])
        for fn, u in plan:
            if u is not None:
                need(u)
            if _DBG.get('trace_stage'):
                _DBG['trace_stage'](getattr(fn, '__qualname__', str(fn)))
            fn()
        P.op("sp", None, reads=[("out", ti) for ti in range(1, NT_ + 1)], inc=0)
        assert not pending
        P.emit(block, sems)
    return nc


def _host_inputs(x, mix_norm_g, w_in, b_in, conv_a_w, conv_a_b, ln_a_g, ln_a_b, ln_b_g, ln_b_b,
                 w_spatial, b_spatial, w_out, ffn_norm_g, w_up, conv_f_w, conv_f_b, w_down, final_norm_g):
    f = lambda v: np.asarray(v, dtype=np.float32)
    x = f(x)
    col = lambda v, n: f(v).reshape(n, 128).T
    vecs = np.zeros((128, NV), np.float32)
    vecs[:, C_G1:C_G1 + 8] = col(mix_norm_g[0], 8)
    vecs[:, C_G2:C_G2 + 8] = col(ffn_norm_g[0], 8)
    vecs[:, C_G3:C_G3 + 8] = col(final_norm_g, 8)
    vecs[:, C_BIN:C_BIN + 12] = col(f(b_in[0])[:1536], 12)
    vecs[:, C_CB:C_CB + 4] = col(conv_a_b[0], 4)
    vecs[:, C_LAG:C_LAG + 4] = col(ln_a_g[0], 4)
    vecs[:, C_LAB:C_LAB + 4] = col(ln_a_b[0], 4)
    cfw = f(conv_f_w[0])
    for tap in range(3):
        vecs[:, C_CFW + tap * 44:C_CFW + (tap + 1) * 44] = col(cfw[tap], 44)
    vecs[:, C_CFB:C_CFB + 44] = col(conv_f_b[0], 44)
    vecs[:, C_ONE] = 1.0
    vecs[:, C_EPSR] = RMS_EPS
    vecs[:, C_EPSL] = LN_EPS
    caw = f(conv_a_w[0])
    for ch in range(4):
        vecs[:, C_CAW + ch * 31:C_CAW + (ch + 1) * 31] = caw[:, ch * 128:(ch + 1) * 128].T
    tiles = np.empty((128, 3, 512), np.float32)
    tiles[:, 0, :] = f(b_in[0])[None, 1536:2048]
    tiles[:, 1, :] = f(ln_b_g[0])[None, :]
    tiles[:, 2, :] = f(ln_b_b[0])[None, :]
    wsT = np.ascontiguousarray(f(w_spatial[0]).transpose(2, 0, 1))
    maskT = np.triu(np.ones((128, 128), np.float32))
    bsr = np.repeat(f(b_spatial[0]), 64, axis=0).reshape(4, 128, 128).transpose(1, 0, 2)
    common = {
        "w_in": np.ascontiguousarray(f(w_in[0])), "w_out": np.ascontiguousarray(f(w_out[0])),
        "w_up": np.ascontiguousarray(f(w_up[0])), "w_down": np.ascontiguousarray(f(w_down[0])),
        "tiles": tiles, "wsT": wsT, "maskT": maskT, "bs": np.ascontiguousarray(bsr),
        "ident": np.eye(128, dtype=np.float32),
    }
    in_maps = []
    for c in range(NCORES):
        b, half = divmod(c, 2)
        t0 = half * TOK
        xt = np.zeros((D, HALO + TOK), np.float32)
        xt[:, HALO:] = x[b, t0:t0 + TOK, :].T
        v = vecs.copy()
        if half:
            xt[:, :HALO] = x[b, t0 - HALO:t0, :].T
            v[:, C_HM] = 1.0
        m = dict(common)
        m["xT"] = xt
        m["vecs"] = v
        in_maps.append(m)
    return in_maps


def kernel(**inputs):
    in_maps = _host_inputs(**inputs)
    nc = build_nc()
    res = run_bass_kernel_spmd(nc, in_maps, core_ids=list(range(NCORES)))
    out = np.empty((4, SEQ, D), np.float32)
    for c in range(NCORES):
        b, half = divmod(c, 2)
        out[b, half * TOK:(half + 1) * TOK, :] = res.results[c]["outT"].T
    return out
```

```python
import numpy as np
from contextlib import ExitStack
import concourse.bass as bass
import concourse.mybir as mybir
from concourse.bass_utils import run_bass_kernel_spmd

F32 = mybir.dt.float32
BF16 = mybir.dt.bfloat16
AF = mybir.ActivationFunctionType
ALU = mybir.AluOpType

NCORES = 8
D = 1024
SEQ = 8192
TOK = 4096
HALO = 128
TT = 512
NT = TOK // TT
DFF = 2816
NFC = DFF // 128
KD = D // 128
RMS_EPS = 1e-6
LN_EPS = 1e-5
NSLOT = 4
NSET = 3
SLOT_EL = 4096
NUNIT = 23

C_G1, C_G2, C_G3 = 0, 8, 16
C_BIN = 24
C_CB = 36
C_LAG = 40
C_LAB = 44
C_CFW = 48
C_CFB = 180
C_HM = 224
C_ONE = 225
C_CAW = 226
C_EPSR = C_CAW + 124
C_EPSL = C_EPSR + 1
NV = C_EPSL + 1


class _OpRec:
    __slots__ = ("seq", "done", "clock")

    def __init__(self, seq, done, clock):
        self.seq = seq
        self.done = done
        self.clock = clock


class _Prog:
    ENG_SEM = {"pe": "s_pe", "act": "s_act", "dve": "s_dve", "pool": "s_pool"}

    def __init__(self):
        self.streams = {e: [] for e in ("pe", "act", "dve", "pool", "sp")}
        self.cnt = {}
        self.clock = {e: {} for e in self.streams}
        self.last_w = {}
        self.readers = {}
        self.seq = 0

    def op(self, eng, fn, reads=(), writes=(), sem=None, inc=1):
        deps = {}
        for k in reads:
            w = self.last_w.get(k)
            if w is not None:
                deps[w.seq] = w
        for k in writes:
            w = self.last_w.get(k)
            if w is not None:
                deps[w.seq] = w
            for r in self.readers.get(k, ()):
                deps[r.seq] = r
        clk = self.clock[eng]
        waits = []
        for s in sorted(deps, reverse=True):
            d = deps[s]
            sn, v = d.done
            if clk.get(sn, 0) >= v:
                continue
            waits.append((sn, v))
            clk[sn] = v
            for a, b in d.clock.items():
                if clk.get(a, 0) < b:
                    clk[a] = b
        semname = sem or self.ENG_SEM.get(eng)
        if inc:
            self.cnt[semname] = self.cnt.get(semname, 0) + inc
            done = (semname, self.cnt[semname])
        else:
            done = ("none", 0)
        self.seq += 1
        rec = _OpRec(self.seq, done, dict(clk))
        self.streams[eng].append((waits, fn, semname, inc))
        for k in writes:
            self.last_w[k] = rec
            self.readers[k] = []
        for k in reads:
            self.readers.setdefault(k, []).append(rec)
        return rec

    def emit(self, block, sems):
        def run(engname):
            def body(eng):
                for waits, fn, semname, inc in self.streams[engname]:
                    for sn, v in waits:
                        eng.wait_ge(sems[sn], v)
                    if fn is None:
                        continue
                    instrs = fn(eng)
                    if not inc:
                        continue
                    if not isinstance(instrs, (list, tuple)):
                        instrs = [instrs]
                    per = inc // len(instrs)
                    assert per * len(instrs) == inc
                    for ins in instrs:
                        ins.then_inc(sems[semname], per)
            return body

        block.tensor(run("pe"))
        block.scalar(run("act"))
        block.vector(run("dve"))
        block.gpsimd(run("pool"))
        block.sync(run("sp"))


def _ACT(**kw):
    return lambda e: e.activation(**kw)


def _TT(**kw):
    return lambda e: e.tensor_tensor(**kw)


def _STT(**kw):
    return lambda e: e.scalar_tensor_tensor(**kw)


def _TS(**kw):
    return lambda e: e.tensor_scalar(**kw)


def _CP(**kw):
    return lambda e: e.tensor_copy(**kw)


def _RCP(**kw):
    return lambda e: e.reciprocal(**kw)


def _MS(ap, val):
    return lambda e: e.memset(ap, val)


def _BNS(**kw):
    return lambda e: e.bn_stats(**kw)


def _BNA(**kw):
    return lambda e: e.bn_aggr(**kw)


def _MM(lst):
    def f(e):
        last = None
        for (o, l, r, st, sp) in lst:
            last = e.matmul(o, l, r, start=st, stop=sp)
        return last
    return f


def _DMA(pairs):
    return lambda e: [e.dma_start(out=o, in_=i) for o, i in pairs]


_DBG = {}


def build_nc(nt=NT):
    TOK = nt * TT
    NT_ = nt
    nc = bass.Bass("TRN2", target_bir_lowering=False)
    dt = nc.dram_tensor
    xT = dt("xT", [D, HALO + TOK], F32, kind="ExternalInput").ap()
    w_in = dt("w_in", [D, 2048], F32, kind="ExternalInput").ap()
    w_out = dt("w_out", [D, D], F32, kind="ExternalInput").ap()
    w_up = dt("w_up", [D, 2 * DFF], F32, kind="ExternalInput").ap()
    w_down = dt("w_down", [DFF, D], F32, kind="ExternalInput").ap()
    vecs_d = dt("vecs", [128, NV], F32, kind="ExternalInput").ap()
    tiles_d = dt("tiles", [128, 3, 512], F32, kind="ExternalInput").ap()
    wsT_d = dt("wsT", [128, 8, 128], F32, kind="ExternalInput").ap()
    mask_d = dt("maskT", [128, 128], F32, kind="ExternalInput").ap()
    bs_d = dt("bs", [128, 4, 128], F32, kind="ExternalInput").ap()
    ident_d = dt("ident", [128, 128], F32, kind="ExternalInput").ap()
    outT = dt("outT", [D, TOK], F32, kind="ExternalOutput").ap()
    scr = dt("scr", [NUNIT, 128, SLOT_EL], BF16, kind="Internal").ap()

    xT_v = xT.rearrange("(k p) t -> p k t", p=128)
    outT_v = outT.rearrange("(k p) t -> p k t", p=128)
    w_in_v = w_in.rearrange("(k p) e -> p k e", p=128)
    w_out_v = w_out.rearrange("(k p) e -> p k e", p=128)
    w_up_v = w_up.rearrange("(k p) e -> p k e", p=128)
    w_dn_v = w_down.rearrange("(c p) e -> p c e", p=128)

    P = _Prog()
    semnames = ["s_pe", "s_act", "s_dve", "s_pool", "cst"]
    semnames += [f"xl{i}_{k}" for i in range(2) for k in (0, 4, 6)] + [f"os{i}_{k}" for i in range(2) for k in range(KD)]
    semnames += [f"rl{i}" for i in range(NSLOT)] + [f"rs{i}" for i in range(NSLOT)] + [f"rc{i}" for i in range(NSLOT)]

    with ExitStack() as es:
        sems = {n: es.enter_context(nc.semaphore(n)) for n in semnames}
        sb = lambda name, shape, dtp: es.enter_context(nc.sbuf_tensor(name, shape, dtp))
        ring = [sb(f"ring{i}", [128, SLOT_EL], BF16) for i in range(NSLOT)]
        xbuf = [sb(f"xb{i}", [128, KD, TT], F32) for i in range(2)]
        y = sb("y", [128, KD, TT], BF16)
        y1 = sb("y1", [128, KD, TT], BF16)
        mx = sb("mx", [128, KD, TT], BF16)
        a = sb("a", [128, 4, 30 + TT], BF16)
        cu = sb("cu", [128, 4, TT], F32)
        vt = sb("vt", [128, 4, 512], F32)
        vlnz = sb("vlnz", [128, 4, 4, 2, 128], BF16)
        wsT = vt[:, 0:2, :].rearrange("p a (b c) -> p (a b) c", c=128)
        st_r = sb("st_r", [128, TT], F32)
        st_mu = sb("st_mu", [128, TT], F32)
        st_t = sb("st_t", [128, TT], F32)
        st_ra = sb("st_ra", [128, TT], F32)
        g = sb("g", [128, NFC, TT], BF16)
        acc = [[sb(f"acc{i}{j}", [128, TT], F32) for j in range(2)] for i in range(NSET)]
        uh = [sb(f"uh{i}", [128, NFC, 2, 2], F32) for i in range(2)]
        bbuf = [sb(f"bb{i}", [128, TT], F32) for i in range(2)]
        corr = sb("corr", [128, NFC, 2, 2], F32)
        ctmp = sb("ctmp", [128, NFC, 2], F32)
        yh = sb("yh", [128, KD, 2], BF16)
        diag = sb("diag", [128, 124, 128], BF16)
        maskT = sb("mask_s", [128, 128], F32)
        wsm = sb("wsm", [128, 8, 128], BF16)
        bs = sb("bs_s", [128, 4, 128], F32)
        tiles = sb("tiles_s", [128, 3, 512], F32)
        ident = sb("ident_s", [128, 128], F32)
        vecs = sb("vecs_s", [128, NV], F32)
        onesm = sb("onesm", [128, 128], BF16)
        ones5 = sb("ones5", [128, 128], BF16)
        bst = sb("bst", [128, 4, 6], F32)
        bag = sb("bag", [128, 4, 2], F32)
        rsb = sb("rsb", [128, 4], F32)
        psb = [es.enter_context(nc.psum_tensor(f"ps{i}", [128, 512], F32)) for i in range(8)]
        block = es.enter_context(nc.Block())

        state = {"bank": 0, "unit_seq": 0, "held": set()}

        def next_bank(hold=False):
            for _ in range(8):
                b = state["bank"]
                state["bank"] = (b + 1) % 8
                if b not in state["held"]:
                    if hold:
                        state["held"].add(b)
                    return b
            raise RuntimeError("all PSUM banks held")

        def release(*banks):
            for b in banks:
                state["held"].discard(b)

        PS = lambda b: ("ps", b)
        vcol = lambda c: vecs[:, c:c + 1]

        P.op("sp", _DMA([(vecs[:], vecs_d), (tiles[:], tiles_d), (wsT, wsT_d), (maskT[:], mask_d),
                         (bs[:], bs_d), (ident[:], ident_d)]),
             writes=["vecs", "tiles", ("vt", 0), ("vt", 1), "maskT", "bs", "ident"], sem="cst", inc=16 * 6)
        P.op("pool", _MS(vlnz[:], 0.0), writes=["vlnz"])
        P.op("pool", _MS(a[:], 0.0), writes=[("a", c) for c in range(4)])
        P.op("dve", _MS(onesm[:], 1.0 / D), writes=["onesm"])
        P.op("dve", _MS(ones5[:], 1.0 / 512), writes=["ones5"])
        for h in range(8):
            P.op("dve", _TT(out=wsm[:, h, :], in0=wsT[:, h, :], in1=maskT[:], op=ALU.mult),
                 reads=[("vt", 0), ("vt", 1), "maskT"], writes=[("wsm", h)])

        def build_diag():
            for i in range(124):
                if i % 2 == 0:
                    P.op("dve", _TS(out=diag[:, i, :], in0=ident[:], scalar1=vcol(C_CAW + i), scalar2=None, op0=ALU.mult),
                         reads=["ident", "vecs"], writes=[("diag", i)])
                else:
                    P.op("act", _ACT(out=diag[:, i, :], in_=ident[:], func=AF.Identity, bias=0.0, scale=vcol(C_CAW + i)),
                         reads=["ident", "vecs"], writes=[("diag", i)])

        in_cols = {0: 512, 1: 0, 2: 1024, 3: 1536}
        dn_groups = [(0, 8), (8, 8), (16, 6)]
        unit_in_scr = set()
        unit_slot = {}
        pending = []

        def unit_src_dmas(u, slot):
            r = ring[slot]
            rv = r[:].rearrange("p (k e) -> p k e", k=KD)
            if u < 4:
                c0 = in_cols[u]
                return [(rv, w_in_v[:, :, c0:c0 + 512])]
            if u < 6:
                c0 = (u - 4) * 512
                return [(rv, w_out_v[:, :, c0:c0 + 512])]
            if u < 17:
                j = u - 6
                res = []
                for pr in range(2):
                    jj = 2 * j + pr
                    res.append((rv[:, :, (2 * pr) * 128:(2 * pr + 1) * 128], w_up_v[:, :, jj * 128:(jj + 1) * 128]))
                    res.append((rv[:, :, (2 * pr + 1) * 128:(2 * pr + 2) * 128],
                                w_up_v[:, :, DFF + jj * 128:DFF + (jj + 1) * 128]))
                return res
            q = u - 17
            sw, gi = q // 3, q % 3
            f0, n = dn_groups[gi]
            return [(r[:, 0:n * 512].rearrange("p (c e) -> p c e", c=n), w_dn_v[:, f0:f0 + n, sw * 512:(sw + 1) * 512])]

        def issue_load(u):
            slot = state["unit_seq"] % NSLOT
            state["unit_seq"] += 1
            unit_slot[u] = slot
            if _DBG.get('noscr') or u not in unit_in_scr:
                pairs = unit_src_dmas(u, slot)
                P.op("pool", _DMA(pairs), writes=[("ring", slot)], sem=f"rc{slot}", inc=16 * len(pairs))
                if not _DBG.get('noscr'):
                    P.op("sp", _DMA([(scr[u], ring[slot][:])]), reads=[("ring", slot)], writes=[("scr", u)],
                         sem=f"rs{slot}", inc=16)
                unit_in_scr.add(u)
            else:
                P.op("sp", _DMA([(ring[slot][:], scr[u])]), reads=[("scr", u)], writes=[("ring", slot)],
                     sem=f"rl{slot}", inc=16)

        def prefetch():
            for ent in pending[:NSLOT - 1]:
                if not ent[1]:
                    issue_load(ent[0])
                    ent[1] = True

        def need(u):
            assert pending and pending[0][0] == u, (u, pending[:3])
            prefetch()
            pending.pop(0)
            prefetch()

        def load_x(ti, c0, T, k0=0, k1=KD):
            s = ti % 2
            P.op("sp", _DMA([(xbuf[s][:, k0:k1, 0:T], xT_v[:, k0:k1, c0:c0 + T])]),
                 writes=[("x", s, k) for k in range(k0, k1)], sem=f"xl{s}_{k0}", inc=16)

        def slotv(slot):
            return ring[slot][:].rearrange("p (k e) -> p k e", k=KD)

        def rms_stages(ti, T, gcol, mode):
            s = ti % 2
            xs = xbuf[s]
            st = {}

            def s_sq():
                for k in range(KD):
                    P.op("act", _ACT(out=mx[:, k, 0:T], in_=xs[:, k, 0:T], func=AF.Square),
                         reads=[("x", s, k)], writes=[("mx", k)])

            def s_stat():
                b = next_bank(hold=True)
                st["b"] = b
                P.op("pe", _MM([(psb[b][:, 0:T], onesm[:], mx[:, k, 0:T], k == 0, k == KD - 1) for k in range(KD)]),
                     reads=[("mx", k) for k in range(KD)] + ["onesm"], writes=[PS(b)])

            def s_y():
                b = st["b"]
                P.op("act", _ACT(out=st_r[:, 0:T], in_=psb[b][:, 0:T], func=AF.Ln, bias=vcol(C_EPSR), scale=1.0),
                     reads=[PS(b), "vecs"], writes=["st_r"])
                release(b)
                P.op("act", _ACT(out=st_r[:, 0:T], in_=st_r[:, 0:T], func=AF.Exp, scale=-0.5), reads=["st_r"], writes=["st_r"])
                for k in range(KD):
                    if mode == "final":
                        if k < 4:
                            dst, key = cu[:, k, 0:T], ("cu", k)
                        else:
                            dst, key = vt[:, k - 4, 0:T], ("vt", k - 4)
                    elif mode == "y1":
                        dst, key = y1[:, k, 0:T], ("y1", k)
                    else:
                        dst, key = y[:, k, 0:T], ("y", k)
                    P.op("dve", _STT(out=dst, in0=xs[:, k, 0:T], scalar=vcol(gcol + k), in1=st_r[:, 0:T],
                                     op0=ALU.mult, op1=ALU.mult),
                         reads=[("x", s, k), "st_r", "vecs"], writes=[key])
            return [(s_sq, None), (s_stat, None), (s_y, None)]

        def mixer_stages(ti, T, is_halo):
            s = ti % 2
            xs = xbuf[s]
            nsub = T // 128
            yk = [("y1", k) for k in range(KD)]
            st = {}

            def s_in(u):
                def f():
                    slot = unit_slot[u]
                    sv = slotv(slot)
                    for ecl in range(4):
                        b = next_bank()
                        P.op("pe", _MM([(psb[b][:, 0:T], sv[:, k, ecl * 128:(ecl + 1) * 128], y1[:, k, 0:T], k == 0, k == KD - 1)
                                        for k in range(KD)]),
                             reads=yk + [("ring", slot)], writes=[PS(b)])
                        if u == 0:
                            P.op("act", _ACT(out=cu[:, ecl, 0:T], in_=psb[b][:, 0:T], func=AF.Sigmoid,
                                             bias=vcol(C_BIN + 4 + ecl), scale=1.0),
                                 reads=[PS(b), "vecs"], writes=[("cu", ecl)])
                        elif u == 1:
                            P.op("dve", _STT(out=a[:, ecl, 30:30 + T], in0=psb[b][:, 0:T], scalar=vcol(C_BIN + ecl),
                                             in1=cu[:, ecl, 0:T], op0=ALU.add, op1=ALU.mult),
                                 reads=[PS(b), "vecs", ("cu", ecl)], writes=[("a", ecl)])
                        else:
                            P.op("act", _ACT(out=cu[:, ecl, 0:T], in_=psb[b][:, 0:T], func=AF.Gelu,
                                             bias=vcol(C_BIN + 8 + ecl), scale=1.0),
                                 reads=[PS(b), "vecs"], writes=[("cu", ecl)])
                return f

            def s_invv():
                slot = unit_slot[3]
                sv = slotv(slot)
                for sidx in range(nsub):
                    b = next_bank()
                    P.op("pe", _MM([(psb[b][:, 0:512], y1[:, k, sidx * 128:(sidx + 1) * 128], sv[:, k, 0:512], k == 0, k == KD - 1)
                                    for k in range(KD)]),
                         reads=yk + [("ring", slot)], writes=[PS(b)])
                    P.op("dve", _TT(out=vt[:, sidx, :], in0=psb[b][:, 0:512], in1=tiles[:, 0, :], op=ALU.add),
                         reads=[PS(b), "tiles"], writes=[("vt", sidx)])
                    P.op("act", _ACT(out=vt[:, sidx, :], in_=vt[:, sidx, :], func=AF.Gelu),
                         reads=[("vt", sidx)], writes=[("vt", sidx)])
                    P.op("dve", _BNS(out=bst[:, sidx, :], in_=vt[:, sidx, :]), reads=[("vt", sidx)], writes=[("bst", sidx)])
                    P.op("dve", _BNA(out=bag[:, sidx, :], in_=bst[:, sidx, :]), reads=[("bst", sidx)], writes=[("bag", sidx)])

            def s_conv(ch):
                def f():
                    b = next_bank()
                    P.op("pe", _MM([(psb[b][:, 0:T], diag[:, ch * 31 + tap, :], a[:, ch, tap:tap + T], tap == 0, tap == 30)
                                    for tap in range(31)]),
                         reads=[("a", ch)] + [("diag", ch * 31 + tap) for tap in range(31)], writes=[PS(b)])
                    P.op("act", _ACT(out=cu[:, ch, 0:T], in_=psb[b][:, 0:T], func=AF.Identity, bias=vcol(C_CB + ch), scale=1.0),
                         reads=[PS(b), "vecs"], writes=[("cu", ch)])
                    P.op("act", _ACT(out=y1[:, ch, 0:T], in_=psb[b][:, 0:T], func=AF.Identity, bias=vcol(C_CB + ch), scale=1.0),
                         reads=[PS(b), "vecs"], writes=[("y1", ch)])
                    P.op("act", _ACT(out=y1[:, 4 + ch, 0:T], in_=psb[b][:, 0:T], func=AF.Square, bias=vcol(C_CB + ch), scale=1.0),
                         reads=[PS(b), "vecs"], writes=[("y1", 4 + ch)])
                return f

            def s_vln():
                P.op("act", _ACT(out=rsb[:, 0:nsub], in_=bag[:, 0:nsub, 1], func=AF.Ln, bias=vcol(C_EPSL), scale=1.0),
                     reads=[("bag", i) for i in range(nsub)] + ["vecs"], writes=["rsb"])
                P.op("act", _ACT(out=rsb[:, 0:nsub], in_=rsb[:, 0:nsub], func=AF.Exp, scale=-0.5), reads=["rsb"], writes=["rsb"])
                for sidx in range(nsub):
                    P.op("dve", _TS(out=vt[:, sidx, :], in0=vt[:, sidx, :], scalar1=bag[:, sidx, 0:1],
                                    scalar2=rsb[:, sidx:sidx + 1], op0=ALU.subtract, op1=ALU.mult),
                         reads=[("vt", sidx), ("bag", sidx), "rsb"], writes=[("vt", sidx)])
                    P.op("dve", _TT(out=vt[:, sidx, :], in0=vt[:, sidx, :], in1=tiles[:, 1, :], op=ALU.mult),
                         reads=[("vt", sidx), "tiles"], writes=[("vt", sidx)])
                    for hh in range(2):
                        P.op("dve", _TT(out=vlnz[:, sidx, :, hh, hh * 64:(hh + 1) * 64],
                                        in0=vt[:, sidx, :].rearrange("p (c q) -> p c q", c=4)[:, :, hh * 64:(hh + 1) * 64],
                                        in1=tiles[:, 2, :].rearrange("p (c q) -> p c q", c=4)[:, :, hh * 64:(hh + 1) * 64],
                                        op=ALU.add),
                             reads=[("vt", sidx), "tiles", "vlnz"], writes=[("vlnz", sidx, hh)])

            def s_sp():
                for ch in range(4):
                    b = next_bank()
                    lst = []
                    for sidx in range(nsub):
                        lst.append((psb[b][:, sidx * 128:(sidx + 1) * 128], vlnz[:, sidx, ch, 0, :], wsm[:, 2 * ch, :], True, False))
                        lst.append((psb[b][:, sidx * 128:(sidx + 1) * 128], vlnz[:, sidx, ch, 1, :], wsm[:, 2 * ch + 1, :], False, True))
                    P.op("pe", _MM(lst),
                         reads=[("vlnz", i, hh) for i in range(nsub) for hh in range(2)] + [("wsm", 2 * ch), ("wsm", 2 * ch + 1)],
                         writes=[PS(b)])
                    for sidx in range(nsub):
                        P.op("dve", _TT(out=st_t[:, sidx * 128:(sidx + 1) * 128], in0=psb[b][:, sidx * 128:(sidx + 1) * 128],
                                        in1=bs[:, ch, :], op=ALU.add),
                             reads=[PS(b), "bs"], writes=["st_t"])
                    P.op("dve", _TT(out=mx[:, 4 + ch, 0:T], in0=st_t[:, 0:T], in1=cu[:, ch, 0:T], op=ALU.mult),
                         reads=["st_t", ("cu", ch)], writes=[("mx", 4 + ch)])

            def s_cev():
                mcol = C_HM if is_halo else C_ONE
                P.op("dve", _TS(out=a[:, :, 0:30], in0=a[:, :, T:T + 30], scalar1=vcol(mcol), scalar2=None, op0=ALU.mult),
                     reads=[("a", c) for c in range(4)] + ["vecs"], writes=[("a", c) for c in range(4)])

            def s_st2():
                bm = next_bank(hold=True)
                be = next_bank(hold=True)
                st["bm"], st["be"] = bm, be
                lst = [(psb[bm][:, 0:T], ones5[:], y1[:, ch, 0:T], ch == 0, ch == 3) for ch in range(4)]
                lst += [(psb[be][:, 0:T], ones5[:], y1[:, 4 + ch, 0:T], ch == 0, ch == 3) for ch in range(4)]
                P.op("pe", _MM(lst), reads=yk + ["ones5"], writes=[PS(bm), PS(be)])

            def s_lna():
                bm, be = st["bm"], st["be"]
                P.op("dve", _CP(out=st_mu[:, 0:T], in_=psb[bm][:, 0:T]), reads=[PS(bm)], writes=["st_mu"])
                P.op("dve", _TT(out=st_t[:, 0:T], in0=st_mu[:, 0:T], in1=st_mu[:, 0:T], op=ALU.mult),
                     reads=["st_mu"], writes=["st_t"])
                P.op("dve", _TT(out=st_t[:, 0:T], in0=psb[be][:, 0:T], in1=st_t[:, 0:T], op=ALU.subtract),
                     reads=[PS(be), "st_t"], writes=["st_t"])
                release(bm, be)
                P.op("act", _ACT(out=st_ra[:, 0:T], in_=st_t[:, 0:T], func=AF.Ln, bias=vcol(C_EPSL), scale=1.0),
                     reads=["st_t", "vecs"], writes=["st_ra"])
                P.op("act", _ACT(out=st_ra[:, 0:T], in_=st_ra[:, 0:T], func=AF.Exp, scale=-0.5), reads=["st_ra"], writes=["st_ra"])
                for ch in range(4):
                    P.op("dve", _TT(out=cu[:, ch, 0:T], in0=cu[:, ch, 0:T], in1=st_mu[:, 0:T], op=ALU.subtract),
                         reads=[("cu", ch), "st_mu"], writes=[("cu", ch)])
                    P.op("dve", _TT(out=cu[:, ch, 0:T], in0=cu[:, ch, 0:T], in1=st_ra[:, 0:T], op=ALU.mult),
                         reads=[("cu", ch), "st_ra"], writes=[("cu", ch)])
                    P.op("act", _ACT(out=mx[:, ch, 0:T], in_=cu[:, ch, 0:T], func=AF.Silu,
                                     bias=vcol(C_LAB + ch), scale=vcol(C_LAG + ch)),
                         reads=[("cu", ch), "vecs"], writes=[("mx", ch)])

            def s_out(u):
                def f():
                    mxk = [("mx", k) for k in range(KD)]
                    slot = unit_slot[u]
                    sv = slotv(slot)
                    for dcl in range(4):
                        dc = (u - 4) * 4 + dcl
                        b = next_bank()
                        P.op("pe", _MM([(psb[b][:, 0:T], sv[:, k, dcl * 128:(dcl + 1) * 128], mx[:, k, 0:T], k == 0, k == KD - 1)
                                        for k in range(KD)]),
                             reads=mxk + [("ring", slot)], writes=[PS(b)])
                        P.op("dve", _TT(out=xs[:, dc, 0:T], in0=xs[:, dc, 0:T], in1=psb[b][:, 0:T], op=ALU.add),
                             reads=[PS(b), ("x", s, dc)], writes=[("x", s, dc)])
                return f

            return [(s_in(0), 0), (s_in(1), 1), (s_in(2), 2), (s_invv, 3), (s_vln, None), (s_sp, None),
                    (s_conv(0), None), (s_conv(1), None), (s_conv(2), None), (s_conv(3), None),
                    (s_cev, None), (s_st2, None), (s_lna, None), (s_out(4), 4), (s_out(5), 5)]

        def ffn_slots(ti, T, first, last_tile):
            s = ti % 2
            xs = xbuf[s]
            cur, nxt = ti % 2, (ti + 1) % 2
            yk = [("y", k) for k in range(KD)]
            res = []
            pend = []

            def s_pair(j, pr):
                def f():
                    slot = unit_slot[6 + j]
                    sv = slotv(slot)
                    jj = 2 * j + pr
                    bg, bv = next_bank(), next_bank()
                    wg = lambda k: sv[:, k, (2 * pr) * 128:(2 * pr + 1) * 128]
                    wv = lambda k: sv[:, k, (2 * pr + 1) * 128:(2 * pr + 2) * 128]
                    lst = [(psb[bg][:, 0:T], wg(k), y[:, k, 0:T], k == 0, k == KD - 1) for k in range(KD)]
                    lst += [(psb[bv][:, 0:T], wv(k), y[:, k, 0:T], k == 0, k == KD - 1) for k in range(KD)]
                    P.op("pe", _MM(lst), reads=yk + [("ring", slot)], writes=[PS(bg), PS(bv)])
                    if first:
                        bh = next_bank()
                        lst = [(psb[bh][:, 0:2], wg(k), yh[:, k, :], k == 0, k == KD - 1) for k in range(KD)]
                        lst += [(psb[bh][:, 2:4], wv(k), yh[:, k, :], k == 0, k == KD - 1) for k in range(KD)]
                        P.op("pe", _MM(lst), reads=["yh", ("ring", slot)], writes=[PS(bh)])
                        P.op("act", _ACT(out=uh[cur][:, jj, :, :], in_=psb[bh][:, 0:4].rearrange("p (a b) -> p a b", a=2),
                                         func=AF.Identity, bias=0.0, scale=vcol(C_HM)),
                             reads=[PS(bh), "vecs"], writes=[("uh", cur, jj)])
                    ag, av = acc[jj % NSET]
                    bbg = bbuf[jj % 2]
                    AK = ("acc", jj % NSET)
                    BK = ("bb", jj % 2)
                    cw = lambda tap, v: vcol(C_CFW + tap * 44 + v * NFC + jj)
                    cb = lambda v: vcol(C_CFB + v * NFC + jj)
                    P.op("act", _ACT(out=ag[:, 0:T], in_=psb[bg][:, 0:T], func=AF.Identity, bias=cb(0), scale=cw(2, 0)),
                         reads=[PS(bg), "vecs"], writes=[AK + (0,)])
                    P.op("act", _ACT(out=bbg[:, 0:T], in_=psb[bg][:, 0:T], func=AF.Identity, bias=0.0, scale=cw(1, 0)),
                         reads=[PS(bg), "vecs"], writes=[BK])
                    P.op("act", _ACT(out=av[:, 0:T], in_=psb[bv][:, 0:T], func=AF.Identity, bias=cb(1), scale=cw(2, 1)),
                         reads=[PS(bv), "vecs"], writes=[AK + (1,)])
                    for v, (ab, pb) in enumerate(((ag, bg), (av, bv))):
                        K = AK + (v,)
                        after_act = [K, BK] if v == 0 else [K]
                        if not last_tile:
                            P.op("dve", _CP(out=uh[nxt][:, jj, v, :], in_=psb[pb][:, T - 2:T]),
                                 reads=[PS(pb)] + after_act,
                                 writes=[("uh", nxt, jj, v)] + ([("uh", nxt, jj)] if ti == 2 else []))
                        P.op("dve", _STT(out=ab[:, 2:T], in0=psb[pb][:, 0:T - 2], scalar=cw(0, v), in1=ab[:, 2:T],
                                         op0=ALU.mult, op1=ALU.add),
                             reads=[PS(pb), "vecs"] + after_act, writes=[K])
                        if v == 1:
                            P.op("dve", _STT(out=ab[:, 1:T], in0=psb[pb][:, 0:T - 1], scalar=cw(1, v), in1=ab[:, 1:T],
                                             op0=ALU.mult, op1=ALU.add),
                                 reads=[PS(pb), "vecs", K], writes=[K])
                        if first:
                            P.op("dve", _STT(out=ab[:, 0:2], in0=uh[cur][:, jj, v, :], scalar=cw(0, v), in1=ab[:, 0:2],
                                             op0=ALU.mult, op1=ALU.add),
                                 reads=[("uh", cur, jj), "vecs", K], writes=[K])
                            P.op("dve", _STT(out=ab[:, 0:1], in0=uh[cur][:, jj, v, 1:2], scalar=cw(1, v), in1=ab[:, 0:1],
                                             op0=ALU.mult, op1=ALU.add),
                                 reads=[("uh", cur, jj), "vecs", K], writes=[K])
                        else:
                            P.op("dve", _TT(out=ab[:, 0:2], in0=ab[:, 0:2], in1=corr[:, jj, v, :], op=ALU.add),
                                 reads=["corr", K], writes=[K])
                        if v == 0:
                            P.op("pool", _TT(out=ab[:, 1:T], in0=ab[:, 1:T], in1=bbg[:, 0:T - 1], op=ALU.add),
                                 reads=[K, BK], writes=[K])

                    def tail(ag=ag, av=av, jj=jj, AK=AK):
                        P.op("act", _ACT(out=ag[:, 0:T], in_=ag[:, 0:T], func=AF.Silu), reads=[AK + (0,)], writes=[AK + (0,)])
                        P.op("pool", _TT(out=g[:, jj, 0:T], in0=ag[:, 0:T], in1=av[:, 0:T], op=ALU.mult),
                             reads=[AK + (0,), AK + (1,)], writes=[("g", jj)])
                    if pend:
                        pend.pop(0)()
                    pend.append(tail)
                    if jj == NFC - 1:
                        pend.pop(0)()
                return f

            def s_corr():
                w0v = vecs[:, C_CFW:C_CFW + 44].rearrange("p (v j) -> p j v", v=2)
                w1v = vecs[:, C_CFW + 44:C_CFW + 88].rearrange("p (v j) -> p j v", v=2)
                uk = [("uh", cur, jj, v) for jj in range(NFC) for v in range(2)]
                P.op("dve", _TT(out=corr[:, :, :, 0], in0=uh[cur][:, :, :, 0], in1=w0v, op=ALU.mult),
                     reads=uk + ["vecs"], writes=["corr"])
                P.op("dve", _TT(out=ctmp[:], in0=uh[cur][:, :, :, 1], in1=w1v, op=ALU.mult),
                     reads=uk + ["vecs"], writes=["ctmp"])
                P.op("dve", _TT(out=corr[:, :, :, 0], in0=corr[:, :, :, 0], in1=ctmp[:], op=ALU.add),
                     reads=["corr", "ctmp"], writes=["corr"])
                P.op("dve", _TT(out=corr[:, :, :, 1], in0=uh[cur][:, :, :, 1], in1=w0v, op=ALU.mult),
                     reads=uk + ["vecs", "corr"], writes=["corr"])

            def s_unit(j):
                f0, f1 = s_pair(j, 0), s_pair(j, 1)

                def f():
                    if j == 0 and not first:
                        s_corr()
                    f0()
                    f1()
                return f

            for j in range(11):
                res.append((s_unit(j), 6 + j))

            dstate = {}

            def s_down(sw, gi):
                def f():
                    if gi == 0:
                        dstate["banks"] = [next_bank(hold=True) for _ in range(4)]
                    banks = dstate["banks"]
                    f0, n = dn_groups[gi]
                    slot = unit_slot[17 + sw * 3 + gi]
                    rv = ring[slot][:, 0:n * 512].rearrange("p (c e) -> p c e", c=n)
                    lst = []
                    for i in range(4):
                        for c in range(n):
                            fc = f0 + c
                            lst.append((psb[banks[i]][:, 0:T], rv[:, c, i * 128:(i + 1) * 128], g[:, fc, 0:T],
                                        fc == 0, fc == NFC - 1))
                    wr = [PS(b) for b in banks] if gi in (0, 2) else []
                    P.op("pe", _MM(lst), reads=[("g", f0 + c) for c in range(n)] + [("ring", slot)], writes=wr)
                    if gi == 2:
                        for i in range(4):
                            dc = sw * 4 + i
                            P.op("dve", _TT(out=xs[:, dc, 0:T], in0=xs[:, dc, 0:T], in1=psb[banks[i]][:, 0:T], op=ALU.add),
                                 reads=[PS(banks[i]), ("x", s, dc)], writes=[("x", s, dc)])
                        release(*banks)
                return f

            for sw in range(2):
                for gi in range(3):
                    res.append((s_down(sw, gi), 17 + sw * 3 + gi))
            return res

        def store_out(ti, c0, T, ks=tuple(range(KD))):
            def f():
                s = ti % 2
                oc = c0 - HALO
                for k in ks:
                    src, key = (cu[:, k, 0:T], ("cu", k)) if k < 4 else (vt[:, k - 4, 0:T], ("vt", k - 4))
                    P.op("sp", _DMA([(outT_v[:, k, oc:oc + T], src)]), reads=[key], writes=[("out", ti, k)],
                         sem=f"os{s}_{k}", inc=16)
            return f

        tiles_sched = [(0, 0, HALO, True)] + [(1 + i, HALO + i * TT, TT, False) for i in range(NT_)]
        plan = []

        def lx(ti, k0=0, k1=KD):
            t = tiles_sched[ti]
            return (lambda: load_x(t[0], t[1], t[2], k0, k1), None)

        plan.append(lx(0))
        plan.append(lx(1))
        for ti in (0, 1):
            _, c0, T, is_halo = tiles_sched[ti]
            plan += rms_stages(ti, T, C_G1, "y1")
            ms = mixer_stages(ti, T, is_halo)
            if ti == 0:
                ms.insert(1, (build_diag, None))
            plan += ms
            plan += rms_stages(ti, T, C_G2, "y2")
            if is_halo:
                plan.append((lambda T=T: P.op("dve", _CP(out=yh[:], in_=y[:, :, T - 2:T]),
                                              reads=[("y", k) for k in range(KD)], writes=["yh"]), None))
                if NT_ >= 2:
                    plan.append(lx(2))
        SLOT_MAP = _DBG.get("slot_map") or [5, 6, 6,
                                            7, 7, 8, 8, 8, 10,
                                            10, 10, 11, 11,
                                            11, 12, 12, 13, 13,
                                            14, 14, 15]
        for ti in range(1, NT_ + 1):
            _, c0, T, _ = tiles_sched[ti]
            slots = ffn_slots(ti, T, first=(ti == 1), last_tile=(ti == NT_))
            inter = {}
            if ti > 1:
                _, pc0, pT, _ = tiles_sched[ti - 1]
                r3 = rms_stages(ti - 1, pT, C_G3, "final")
                inter.setdefault(0, []).append(r3[0])
                inter.setdefault(1, []).append(r3[1])
                inter.setdefault(1, []).append(r3[2])
                if ti + 1 <= NT_:
                    inter[1].append(lx(ti + 1, 0, 4))
                    inter[1].append(lx(ti + 1, 4, 8))
                for i_, sl_ in enumerate((3, 4, 5, 6)):
                    inter.setdefault(sl_, []).append((store_out(ti - 1, pc0, pT, (i_, 4 + i_)), None))
            if ti + 1 <= NT_:
                _, nc0, nT, _ = tiles_sched[ti + 1]
                stg = rms_stages(ti + 1, nT, C_G1, "y1") + mixer_stages(ti + 1, nT, False) + rms_stages(ti + 1, nT, C_G2, "y2")
                assert len(stg) == len(SLOT_MAP)
                for st_, sl in zip(stg, SLOT_MAP):
                    inter.setdefault(sl, []).append(st_)
            for si, ent in enumerate(slots):
                plan.append(ent)
                plan.extend(inter.get(si, []))
        _, lc0, lT, _ = tiles_sched[NT_]
        plan += rms_stages(NT_, lT, C_G3, "final")
        plan.append((store_out(NT_, lc0, lT), None))

        for fn, u in plan:
            if u is not None:
                pending.append([u, False])
        for fn, u in plan:
            if u is not None:
                need(u)
            if _DBG.get('trace_stage'):
                _DBG['trace_stage'](getattr(fn, '__qualname__', str(fn)))
            fn()
        P.op("sp", None, reads=[("out", ti, k) for ti in range(1, NT_ + 1) for k in range(KD)], inc=0)
        assert not pending
        P.emit(block, sems)
    return nc


def _host_inputs(x, mix_norm_g, w_in, b_in, conv_a_w, conv_a_b, ln_a_g, ln_a_b, ln_b_g, ln_b_b,
                 w_spatial, b_spatial, w_out, ffn_norm_g, w_up, conv_f_w, conv_f_b, w_down, final_norm_g):
    f = lambda v: np.asarray(v, dtype=np.float32)
    x = f(x)
    col = lambda v, n: f(v).reshape(n, 128).T
    vecs = np.zeros((128, NV), np.float32)
    vecs[:, C_G1:C_G1 + 8] = col(mix_norm_g[0], 8)
    vecs[:, C_G2:C_G2 + 8] = col(ffn_norm_g[0], 8)
    vecs[:, C_G3:C_G3 + 8] = col(final_norm_g, 8)
    vecs[:, C_BIN:C_BIN + 12] = col(f(b_in[0])[:1536], 12)
    vecs[:, C_CB:C_CB + 4] = col(conv_a_b[0], 4)
    vecs[:, C_LAG:C_LAG + 4] = col(ln_a_g[0], 4)
    vecs[:, C_LAB:C_LAB + 4] = col(ln_a_b[0], 4)
    cfw = f(conv_f_w[0])
    for tap in range(3):
        vecs[:, C_CFW + tap * 44:C_CFW + (tap + 1) * 44] = col(cfw[tap], 44)
    vecs[:, C_CFB:C_CFB + 44] = col(conv_f_b[0], 44)
    vecs[:, C_ONE] = 1.0
    vecs[:, C_EPSR] = RMS_EPS
    vecs[:, C_EPSL] = LN_EPS
    caw = f(conv_a_w[0])
    for ch in range(4):
        vecs[:, C_CAW + ch * 31:C_CAW + (ch + 1) * 31] = caw[:, ch * 128:(ch + 1) * 128].T
    tiles = np.empty((128, 3, 512), np.float32)
    tiles[:, 0, :] = f(b_in[0])[None, 1536:2048]
    tiles[:, 1, :] = f(ln_b_g[0])[None, :]
    tiles[:, 2, :] = f(ln_b_b[0])[None, :]
    wsT = np.ascontiguousarray(f(w_spatial[0]).transpose(2, 0, 1))
    maskT = np.triu(np.ones((128, 128), np.float32))
    bsr = np.repeat(f(b_spatial[0]), 64, axis=0).reshape(4, 128, 128).transpose(1, 0, 2)
    common = {
        "w_in": np.ascontiguousarray(f(w_in[0])), "w_out": np.ascontiguousarray(f(w_out[0])),
        "w_up": np.ascontiguousarray(f(w_up[0])), "w_down": np.ascontiguousarray(f(w_down[0])),
        "tiles": tiles, "wsT": wsT, "maskT": maskT, "bs": np.ascontiguousarray(bsr),
        "ident": np.eye(128, dtype=np.float32),
    }
    in_maps = []
    for c in range(NCORES):
        b, half = divmod(c, 2)
        t0 = half * TOK
        xt = np.zeros((D, HALO + TOK), np.float32)
        xt[:, HALO:] = x[b, t0:t0 + TOK, :].T
        v = vecs.copy()
        if half:
            xt[:, :HALO] = x[b, t0 - HALO:t0, :].T
            v[:, C_HM] = 1.0
        m = dict(common)
        m["xT"] = xt
        m["vecs"] = v
        in_maps.append(m)
    return in_maps


def kernel(**inputs):
    in_maps = _host_inputs(**inputs)
    nc = build_nc()
    res = run_bass_kernel_spmd(nc, in_maps, core_ids=list(range(NCORES)))
    out = np.empty((4, SEQ, D), np.float32)
    for c in range(NCORES):
        b, half = divmod(c, 2)
        out[b, half * TOK:(half + 1) * TOK, :] = res.results[c]["outT"].T
    return out
```

```python
import numpy as np
from contextlib import ExitStack
import concourse.bass as bass
import concourse.mybir as mybir
from concourse.bass_utils import run_bass_kernel_spmd

F32 = mybir.dt.float32
BF16 = mybir.dt.bfloat16
AF = mybir.ActivationFunctionType
ALU = mybir.AluOpType

NCORES = 8
D = 1024
SEQ = 8192
TOK = 4096
HALO = 128
TT = 512
NT = TOK // TT
DFF = 2816
NFC = DFF // 128
KD = D // 128
RMS_EPS = 1e-6
LN_EPS = 1e-5
NSLOT = 4
NSET = 3
SLOT_EL = 4096
NUNIT = 23

C_G1, C_G2, C_G3 = 0, 8, 16
C_BIN = 24
C_CB = 36
C_LAG = 40
C_LAB = 44
C_CFW = 48
C_CFB = 180
C_HM = 224
C_ONE = 225
C_CAW = 226
C_EPSR = C_CAW + 124
C_EPSL = C_EPSR + 1
NV = C_EPSL + 1


class _OpRec:
    __slots__ = ("seq", "done", "clock")

    def __init__(self, seq, done, clock):
        self.seq = seq
        self.done = done
        self.clock = clock


class _Prog:
    ENG_SEM = {"pe": "s_pe", "act": "s_act", "dve": "s_dve", "pool": "s_pool"}

    def __init__(self):
        self.streams = {e: [] for e in ("pe", "act", "dve", "pool", "sp")}
        self.cnt = {}
        self.clock = {e: {} for e in self.streams}
        self.last_w = {}
        self.readers = {}
        self.seq = 0

    def op(self, eng, fn, reads=(), writes=(), sem=None, inc=1):
        deps = {}
        for k in reads:
            w = self.last_w.get(k)
            if w is not None:
                deps[w.seq] = w
        for k in writes:
            w = self.last_w.get(k)
            if w is not None:
                deps[w.seq] = w
            for r in self.readers.get(k, ()):
                deps[r.seq] = r
        clk = self.clock[eng]
        waits = []
        for s in sorted(deps, reverse=True):
            d = deps[s]
            sn, v = d.done
            if clk.get(sn, 0) >= v:
                continue
            waits.append((sn, v))
            clk[sn] = v
            for a, b in d.clock.items():
                if clk.get(a, 0) < b:
                    clk[a] = b
        semname = sem or self.ENG_SEM.get(eng)
        if inc:
            self.cnt[semname] = self.cnt.get(semname, 0) + inc
            done = (semname, self.cnt[semname])
        else:
            done = ("none", 0)
        self.seq += 1
        rec = _OpRec(self.seq, done, dict(clk))
        self.streams[eng].append((waits, fn, semname, inc))
        for k in writes:
            self.last_w[k] = rec
            self.readers[k] = []
        for k in reads:
            self.readers.setdefault(k, []).append(rec)
        return rec

    def emit(self, block, sems):
        def run(engname):
            def body(eng):
                for waits, fn, semname, inc in self.streams[engname]:
                    for sn, v in waits:
                        eng.wait_ge(sems[sn], v)
                    if fn is None:
                        continue
                    instrs = fn(eng)
                    if not inc:
                        continue
                    if not isinstance(instrs, (list, tuple)):
                        instrs = [instrs]
                    per = inc // len(instrs)
                    assert per * len(instrs) == inc
                    for ins in instrs:
                        ins.then_inc(sems[semname], per)
            return body

        block.tensor(run("pe"))
        block.scalar(run("act"))
        block.vector(run("dve"))
        block.gpsimd(run("pool"))
        block.sync(run("sp"))


def _ACT(**kw):
    return lambda e: e.activation(**kw)


def _TT(**kw):
    return lambda e: e.tensor_tensor(**kw)


def _STT(**kw):
    return lambda e: e.scalar_tensor_tensor(**kw)


def _TS(**kw):
    return lambda e: e.tensor_scalar(**kw)


def _CP(**kw):
    return lambda e: e.tensor_copy(**kw)


def _RCP(**kw):
    return lambda e: e.reciprocal(**kw)


def _MS(ap, val):
    return lambda e: e.memset(ap, val)


def _BNS(**kw):
    return lambda e: e.bn_stats(**kw)


def _BNA(**kw):
    return lambda e: e.bn_aggr(**kw)


def _MM(lst):
    def f(e):
        last = None
        for (o, l, r, st, sp) in lst:
            last = e.matmul(o, l, r, start=st, stop=sp)
        return last
    return f


def _DMA(pairs):
    return lambda e: [e.dma_start(out=o, in_=i) for o, i in pairs]


_DBG = {}


def build_nc(nt=NT):
    TOK = nt * TT
    NT_ = nt
    nc = bass.Bass("TRN2", target_bir_lowering=False)
    dt = nc.dram_tensor
    xT = dt("xT", [D, HALO + TOK], F32, kind="ExternalInput").ap()
    w_in = dt("w_in", [D, 2048], F32, kind="ExternalInput").ap()
    w_out = dt("w_out", [D, D], F32, kind="ExternalInput").ap()
    w_up = dt("w_up", [D, 2 * DFF], F32, kind="ExternalInput").ap()
    w_down = dt("w_down", [DFF, D], F32, kind="ExternalInput").ap()
    vecs_d = dt("vecs", [128, NV], F32, kind="ExternalInput").ap()
    tiles_d = dt("tiles", [128, 3, 512], F32, kind="ExternalInput").ap()
    wsT_d = dt("wsT", [128, 8, 128], F32, kind="ExternalInput").ap()
    mask_d = dt("maskT", [128, 128], F32, kind="ExternalInput").ap()
    bs_d = dt("bs", [128, 4, 128], F32, kind="ExternalInput").ap()
    ident_d = dt("ident", [128, 128], F32, kind="ExternalInput").ap()
    outT = dt("outT", [D, TOK], F32, kind="ExternalOutput").ap()
    scr = dt("scr", [NUNIT, 128, SLOT_EL], BF16, kind="Internal").ap()

    xT_v = xT.rearrange("(k p) t -> p k t", p=128)
    outT_v = outT.rearrange("(k p) t -> p k t", p=128)
    w_in_v = w_in.rearrange("(k p) e -> p k e", p=128)
    w_out_v = w_out.rearrange("(k p) e -> p k e", p=128)
    w_up_v = w_up.rearrange("(k p) e -> p k e", p=128)
    w_dn_v = w_down.rearrange("(c p) e -> p c e", p=128)

    P = _Prog()
    semnames = ["s_pe", "s_act", "s_dve", "s_pool", "cst"]
    semnames += [f"xl{i}_{k}" for i in range(2) for k in (0, 4, 6)] + [f"os{i}_{k}" for i in range(2) for k in range(KD)]
    semnames += [f"rl{i}" for i in range(NSLOT)] + [f"rs{i}" for i in range(NSLOT)] + [f"rc{i}" for i in range(NSLOT)]

    with ExitStack() as es:
        sems = {n: es.enter_context(nc.semaphore(n)) for n in semnames}
        sb = lambda name, shape, dtp: es.enter_context(nc.sbuf_tensor(name, shape, dtp))
        ring = [sb(f"ring{i}", [128, SLOT_EL], BF16) for i in range(NSLOT)]
        xbuf = [sb(f"xb{i}", [128, KD, TT], F32) for i in range(2)]
        y = sb("y", [128, KD, TT], BF16)
        y1 = sb("y1", [128, KD, TT], BF16)
        mx = sb("mx", [128, KD, TT], BF16)
        a = sb("a", [128, 4, 30 + TT], BF16)
        cu = sb("cu", [128, 4, TT], F32)
        vt = sb("vt", [128, 4, 512], F32)
        vlnz = sb("vlnz", [128, 4, 4, 2, 128], BF16)
        wsT = vt[:, 0:2, :].rearrange("p a (b c) -> p (a b) c", c=128)
        st_r = sb("st_r", [128, TT], F32)
        st_mu = sb("st_mu", [128, TT], F32)
        st_t = sb("st_t", [128, TT], F32)
        st_ra = sb("st_ra", [128, TT], F32)
        g = sb("g", [128, NFC, TT], BF16)
        acc = [[sb(f"acc{i}{j}", [128, TT], F32) for j in range(2)] for i in range(NSET)]
        uh = [sb(f"uh{i}", [128, NFC, 2, 2], F32) for i in range(2)]
        bbuf = [sb(f"bb{i}", [128, TT], F32) for i in range(2)]
        corr = sb("corr", [128, NFC, 2, 2], F32)
        ctmp = sb("ctmp", [128, NFC, 2], F32)
        yh = sb("yh", [128, KD, 2], BF16)
        diag = sb("diag", [128, 124, 128], BF16)
        maskT = sb("mask_s", [128, 128], F32)
        wsm = sb("wsm", [128, 8, 128], BF16)
        bs = sb("bs_s", [128, 4, 128], F32)
        tiles = sb("tiles_s", [128, 3, 512], F32)
        ident = sb("ident_s", [128, 128], F32)
        vecs = sb("vecs_s", [128, NV], F32)
        onesm = sb("onesm", [128, 128], BF16)
        ones5 = sb("ones5", [128, 128], BF16)
        bst = sb("bst", [128, 4, 6], F32)
        bag = sb("bag", [128, 4, 2], F32)
        rsb = sb("rsb", [128, 4], F32)
        psb = [es.enter_context(nc.psum_tensor(f"ps{i}", [128, 512], F32)) for i in range(8)]
        block = es.enter_context(nc.Block())

        state = {"bank": 0, "unit_seq": 0, "held": set()}

        def next_bank(hold=False):
            for _ in range(8):
                b = state["bank"]
                state["bank"] = (b + 1) % 8
                if b not in state["held"]:
                    if hold:
                        state["held"].add(b)
                    return b
            raise RuntimeError("all PSUM banks held")

        def release(*banks):
            for b in banks:
                state["held"].discard(b)

        PS = lambda b: ("ps", b)
        vcol = lambda c: vecs[:, c:c + 1]

        P.op("sp", _DMA([(vecs[:], vecs_d), (tiles[:], tiles_d), (wsT, wsT_d), (maskT[:], mask_d),
                         (bs[:], bs_d), (ident[:], ident_d)]),
             writes=["vecs", "tiles", ("vt", 0), ("vt", 1), "maskT", "bs", "ident"], sem="cst", inc=16 * 6)
        P.op("pool", _MS(vlnz[:], 0.0), writes=["vlnz"])
        P.op("pool", _MS(a[:], 0.0), writes=[("a", c) for c in range(4)])
        P.op("dve", _MS(onesm[:], 1.0 / D), writes=["onesm"])
        P.op("dve", _MS(ones5[:], 1.0 / 512), writes=["ones5"])
        for h in range(8):
            P.op("dve", _TT(out=wsm[:, h, :], in0=wsT[:, h, :], in1=maskT[:], op=ALU.mult),
                 reads=[("vt", 0), ("vt", 1), "maskT"], writes=[("wsm", h)])

        def build_diag():
            for i in range(124):
                if i % 2 == 0:
                    P.op("dve", _TS(out=diag[:, i, :], in0=ident[:], scalar1=vcol(C_CAW + i), scalar2=None, op0=ALU.mult),
                         reads=["ident", "vecs"], writes=[("diag", i)])
                else:
                    P.op("act", _ACT(out=diag[:, i, :], in_=ident[:], func=AF.Identity, bias=0.0, scale=vcol(C_CAW + i)),
                         reads=["ident", "vecs"], writes=[("diag", i)])

        in_cols = {0: 512, 1: 0, 2: 1024, 3: 1536}
        dn_groups = [(0, 8), (8, 8), (16, 6)]
        unit_in_scr = set()
        unit_slot = {}
        pending = []

        def unit_src_dmas(u, slot):
            r = ring[slot]
            rv = r[:].rearrange("p (k e) -> p k e", k=KD)
            if u < 4:
                c0 = in_cols[u]
                return [(rv, w_in_v[:, :, c0:c0 + 512])]
            if u < 6:
                c0 = (u - 4) * 512
                return [(rv, w_out_v[:, :, c0:c0 + 512])]
            if u < 17:
                j = u - 6
                res = []
                for pr in range(2):
                    jj = 2 * j + pr
                    res.append((rv[:, :, (2 * pr) * 128:(2 * pr + 1) * 128], w_up_v[:, :, jj * 128:(jj + 1) * 128]))
                    res.append((rv[:, :, (2 * pr + 1) * 128:(2 * pr + 2) * 128],
                                w_up_v[:, :, DFF + jj * 128:DFF + (jj + 1) * 128]))
                return res
            q = u - 17
            sw, gi = q // 3, q % 3
            f0, n = dn_groups[gi]
            return [(r[:, 0:n * 512].rearrange("p (c e) -> p c e", c=n), w_dn_v[:, f0:f0 + n, sw * 512:(sw + 1) * 512])]

        def issue_load(u):
            slot = state["unit_seq"] % NSLOT
            state["unit_seq"] += 1
            unit_slot[u] = slot
            if _DBG.get('noscr') or u not in unit_in_scr:
                pairs = unit_src_dmas(u, slot)
                P.op("pool", _DMA(pairs), writes=[("ring", slot)], sem=f"rc{slot}", inc=16 * len(pairs))
                if not _DBG.get('noscr'):
                    P.op("sp", _DMA([(scr[u], ring[slot][:])]), reads=[("ring", slot)], writes=[("scr", u)],
                         sem=f"rs{slot}", inc=16)
                unit_in_scr.add(u)
            else:
                P.op("sp", _DMA([(ring[slot][:], scr[u])]), reads=[("scr", u)], writes=[("ring", slot)],
                     sem=f"rl{slot}", inc=16)

        def prefetch():
            for ent in pending[:NSLOT - 1]:
                if not ent[1]:
                    issue_load(ent[0])
                    ent[1] = True

        def need(u):
            assert pending and pending[0][0] == u, (u, pending[:3])
            prefetch()
            pending.pop(0)
            prefetch()

        def load_x(ti, c0, T, k0=0, k1=KD):
            s = ti % 2
            P.op("sp", _DMA([(xbuf[s][:, k0:k1, 0:T], xT_v[:, k0:k1, c0:c0 + T])]),
                 writes=[("x", s, k) for k in range(k0, k1)], sem=f"xl{s}_{k0}", inc=16)

        def slotv(slot):
            return ring[slot][:].rearrange("p (k e) -> p k e", k=KD)

        def rms_stages(ti, T, gcol, mode):
            s = ti % 2
            xs = xbuf[s]
            st = {}

            def s_sq():
                for k in range(KD):
                    P.op("act", _ACT(out=mx[:, k, 0:T], in_=xs[:, k, 0:T], func=AF.Square),
                         reads=[("x", s, k)], writes=[("mx", k)])

            def s_stat():
                b = next_bank(hold=True)
                st["b"] = b
                P.op("pe", _MM([(psb[b][:, 0:T], onesm[:], mx[:, k, 0:T], k == 0, k == KD - 1) for k in range(KD)]),
                     reads=[("mx", k) for k in range(KD)] + ["onesm"], writes=[PS(b)])

            def s_y():
                b = st["b"]
                P.op("act", _ACT(out=st_r[:, 0:T], in_=psb[b][:, 0:T], func=AF.Ln, bias=vcol(C_EPSR), scale=1.0),
                     reads=[PS(b), "vecs"], writes=["st_r"])
                release(b)
                P.op("act", _ACT(out=st_r[:, 0:T], in_=st_r[:, 0:T], func=AF.Exp, scale=-0.5), reads=["st_r"], writes=["st_r"])
                for k in range(KD):
                    if mode == "final":
                        if k < 4:
                            dst, key = cu[:, k, 0:T], ("cu", k)
                        else:
                            dst, key = vt[:, k - 4, 0:T], ("vt", k - 4)
                    elif mode == "y1":
                        dst, key = y1[:, k, 0:T], ("y1", k)
                    else:
                        dst, key = y[:, k, 0:T], ("y", k)
                    P.op("dve", _STT(out=dst, in0=xs[:, k, 0:T], scalar=vcol(gcol + k), in1=st_r[:, 0:T],
                                     op0=ALU.mult, op1=ALU.mult),
                         reads=[("x", s, k), "st_r", "vecs"], writes=[key])
            return [(s_sq, None), (s_stat, None), (s_y, None)]

        def mixer_stages(ti, T, is_halo):
            s = ti % 2
            xs = xbuf[s]
            nsub = T // 128
            yk = [("y1", k) for k in range(KD)]
            st = {}

            def s_in(u):
                def f():
                    slot = unit_slot[u]
                    sv = slotv(slot)
                    for ecl in range(4):
                        b = next_bank()
                        P.op("pe", _MM([(psb[b][:, 0:T], sv[:, k, ecl * 128:(ecl + 1) * 128], y1[:, k, 0:T], k == 0, k == KD - 1)
                                        for k in range(KD)]),
                             reads=yk + [("ring", slot)], writes=[PS(b)])
                        if u == 0:
                            P.op("act", _ACT(out=cu[:, ecl, 0:T], in_=psb[b][:, 0:T], func=AF.Sigmoid,
                                             bias=vcol(C_BIN + 4 + ecl), scale=1.0),
                                 reads=[PS(b), "vecs"], writes=[("cu", ecl)])
                        elif u == 1:
                            P.op("dve", _STT(out=a[:, ecl, 30:30 + T], in0=psb[b][:, 0:T], scalar=vcol(C_BIN + ecl),
                                             in1=cu[:, ecl, 0:T], op0=ALU.add, op1=ALU.mult),
                                 reads=[PS(b), "vecs", ("cu", ecl)], writes=[("a", ecl)])
                        else:
                            P.op("act", _ACT(out=cu[:, ecl, 0:T], in_=psb[b][:, 0:T], func=AF.Gelu,
                                             bias=vcol(C_BIN + 8 + ecl), scale=1.0),
                                 reads=[PS(b), "vecs"], writes=[("cu", ecl)])
                return f

            def s_invv():
                slot = unit_slot[3]
                sv = slotv(slot)
                for sidx in range(nsub):
                    b = next_bank()
                    P.op("pe", _MM([(psb[b][:, 0:512], y1[:, k, sidx * 128:(sidx + 1) * 128], sv[:, k, 0:512], k == 0, k == KD - 1)
                                    for k in range(KD)]),
                         reads=yk + [("ring", slot)], writes=[PS(b)])
                    P.op("dve", _TT(out=vt[:, sidx, :], in0=psb[b][:, 0:512], in1=tiles[:, 0, :], op=ALU.add),
                         reads=[PS(b), "tiles"], writes=[("vt", sidx)])
                    P.op("act", _ACT(out=vt[:, sidx, :], in_=vt[:, sidx, :], func=AF.Gelu),
                         reads=[("vt", sidx)], writes=[("vt", sidx)])
                    P.op("dve", _BNS(out=bst[:, sidx, :], in_=vt[:, sidx, :]), reads=[("vt", sidx)], writes=[("bst", sidx)])
                    P.op("dve", _BNA(out=bag[:, sidx, :], in_=bst[:, sidx, :]), reads=[("bst", sidx)], writes=[("bag", sidx)])

            def s_conv(ch):
                def f():
                    b = next_bank()
                    P.op("pe", _MM([(psb[b][:, 0:T], diag[:, ch * 31 + tap, :], a[:, ch, tap:tap + T], tap == 0, tap == 30)
                                    for tap in range(31)]),
                         reads=[("a", ch)] + [("diag", ch * 31 + tap) for tap in range(31)], writes=[PS(b)])
                    P.op("act", _ACT(out=cu[:, ch, 0:T], in_=psb[b][:, 0:T], func=AF.Identity, bias=vcol(C_CB + ch), scale=1.0),
                         reads=[PS(b), "vecs"], writes=[("cu", ch)])
                    P.op("act", _ACT(out=y1[:, ch, 0:T], in_=psb[b][:, 0:T], func=AF.Identity, bias=vcol(C_CB + ch), scale=1.0),
                         reads=[PS(b), "vecs"], writes=[("y1", ch)])
                    P.op("act", _ACT(out=y1[:, 4 + ch, 0:T], in_=psb[b][:, 0:T], func=AF.Square, bias=vcol(C_CB + ch), scale=1.0),
                         reads=[PS(b), "vecs"], writes=[("y1", 4 + ch)])
                return f

            def s_vln():
                P.op("act", _ACT(out=rsb[:, 0:nsub], in_=bag[:, 0:nsub, 1], func=AF.Ln, bias=vcol(C_EPSL), scale=1.0),
                     reads=[("bag", i) for i in range(nsub)] + ["vecs"], writes=["rsb"])
                P.op("act", _ACT(out=rsb[:, 0:nsub], in_=rsb[:, 0:nsub], func=AF.Exp, scale=-0.5), reads=["rsb"], writes=["rsb"])
                for sidx in range(nsub):
                    P.op("dve", _TS(out=vt[:, sidx, :], in0=vt[:, sidx, :], scalar1=bag[:, sidx, 0:1],
                                    scalar2=rsb[:, sidx:sidx + 1], op0=ALU.subtract, op1=ALU.mult),
                         reads=[("vt", sidx), ("bag", sidx), "rsb"], writes=[("vt", sidx)])
                    P.op("dve", _TT(out=vt[:, sidx, :], in0=vt[:, sidx, :], in1=tiles[:, 1, :], op=ALU.mult),
                         reads=[("vt", sidx), "tiles"], writes=[("vt", sidx)])
                    for hh in range(2):
                        P.op("dve", _TT(out=vlnz[:, sidx, :, hh, hh * 64:(hh + 1) * 64],
                                        in0=vt[:, sidx, :].rearrange("p (c q) -> p c q", c=4)[:, :, hh * 64:(hh + 1) * 64],
                                        in1=tiles[:, 2, :].rearrange("p (c q) -> p c q", c=4)[:, :, hh * 64:(hh + 1) * 64],
                                        op=ALU.add),
                             reads=[("vt", sidx), "tiles", "vlnz"], writes=[("vlnz", sidx, hh)])

            def s_sp():
                for ch in range(4):
                    b = next_bank()
                    lst = []
                    for sidx in range(nsub):
                        lst.append((psb[b][:, sidx * 128:(sidx + 1) * 128], vlnz[:, sidx, ch, 0, :], wsm[:, 2 * ch, :], True, False))
                        lst.append((psb[b][:, sidx * 128:(sidx + 1) * 128], vlnz[:, sidx, ch, 1, :], wsm[:, 2 * ch + 1, :], False, True))
                    P.op("pe", _MM(lst),
                         reads=[("vlnz", i, hh) for i in range(nsub) for hh in range(2)] + [("wsm", 2 * ch), ("wsm", 2 * ch + 1)],
                         writes=[PS(b)])
                    for sidx in range(nsub):
                        P.op("dve", _TT(out=st_t[:, sidx * 128:(sidx + 1) * 128], in0=psb[b][:, sidx * 128:(sidx + 1) * 128],
                                        in1=bs[:, ch, :], op=ALU.add),
                             reads=[PS(b), "bs"], writes=["st_t"])
                    P.op("dve", _TT(out=mx[:, 4 + ch, 0:T], in0=st_t[:, 0:T], in1=cu[:, ch, 0:T], op=ALU.mult),
                         reads=["st_t", ("cu", ch)], writes=[("mx", 4 + ch)])

            def s_cev():
                mcol = C_HM if is_halo else C_ONE
                P.op("dve", _TS(out=a[:, :, 0:30], in0=a[:, :, T:T + 30], scalar1=vcol(mcol), scalar2=None, op0=ALU.mult),
                     reads=[("a", c) for c in range(4)] + ["vecs"], writes=[("a", c) for c in range(4)])

            def s_st2():
                bm = next_bank(hold=True)
                be = next_bank(hold=True)
                st["bm"], st["be"] = bm, be
                lst = [(psb[bm][:, 0:T], ones5[:], y1[:, ch, 0:T], ch == 0, ch == 3) for ch in range(4)]
                lst += [(psb[be][:, 0:T], ones5[:], y1[:, 4 + ch, 0:T], ch == 0, ch == 3) for ch in range(4)]
                P.op("pe", _MM(lst), reads=yk + ["ones5"], writes=[PS(bm), PS(be)])

            def s_lna():
                bm, be = st["bm"], st["be"]
                P.op("dve", _CP(out=st_mu[:, 0:T], in_=psb[bm][:, 0:T]), reads=[PS(bm)], writes=["st_mu"])
                P.op("dve", _TT(out=st_t[:, 0:T], in0=st_mu[:, 0:T], in1=st_mu[:, 0:T], op=ALU.mult),
                     reads=["st_mu"], writes=["st_t"])
                P.op("dve", _TT(out=st_t[:, 0:T], in0=psb[be][:, 0:T], in1=st_t[:, 0:T], op=ALU.subtract),
                     reads=[PS(be), "st_t"], writes=["st_t"])
                release(bm, be)
                P.op("act", _ACT(out=st_ra[:, 0:T], in_=st_t[:, 0:T], func=AF.Ln, bias=vcol(C_EPSL), scale=1.0),
                     reads=["st_t", "vecs"], writes=["st_ra"])
                P.op("act", _ACT(out=st_ra[:, 0:T], in_=st_ra[:, 0:T], func=AF.Exp, scale=-0.5), reads=["st_ra"], writes=["st_ra"])
                for ch in range(4):
                    P.op("dve", _TT(out=cu[:, ch, 0:T], in0=cu[:, ch, 0:T], in1=st_mu[:, 0:T], op=ALU.subtract),
                         reads=[("cu", ch), "st_mu"], writes=[("cu", ch)])
                    P.op("dve", _TT(out=cu[:, ch, 0:T], in0=cu[:, ch, 0:T], in1=st_ra[:, 0:T], op=ALU.mult),
                         reads=[("cu", ch), "st_ra"], writes=[("cu", ch)])
                    P.op("act", _ACT(out=mx[:, ch, 0:T], in_=cu[:, ch, 0:T], func=AF.Silu,
                                     bias=vcol(C_LAB + ch), scale=vcol(C_LAG + ch)),
                         reads=[("cu", ch), "vecs"], writes=[("mx", ch)])

            def s_out(u):
                def f():
                    mxk = [("mx", k) for k in range(KD)]
                    slot = unit_slot[u]
                    sv = slotv(slot)
                    for dcl in range(4):
                        dc = (u - 4) * 4 + dcl
                        b = next_bank()
                        P.op("pe", _MM([(psb[b][:, 0:T], sv[:, k, dcl * 128:(dcl + 1) * 128], mx[:, k, 0:T], k == 0, k == KD - 1)
                                        for k in range(KD)]),
                             reads=mxk + [("ring", slot)], writes=[PS(b)])
                        P.op("dve", _TT(out=xs[:, dc, 0:T], in0=xs[:, dc, 0:T], in1=psb[b][:, 0:T], op=ALU.add),
                             reads=[PS(b), ("x", s, dc)], writes=[("x", s, dc)])
                return f

            return [(s_in(0), 0), (s_in(1), 1), (s_in(2), 2), (s_invv, 3), (s_vln, None), (s_sp, None),
                    (s_conv(0), None), (s_conv(1), None), (s_conv(2), None), (s_conv(3), None),
                    (s_cev, None), (s_st2, None), (s_lna, None), (s_out(4), 4), (s_out(5), 5)]

        def ffn_slots(ti, T, first, last_tile):
            s = ti % 2
            xs = xbuf[s]
            cur, nxt = ti % 2, (ti + 1) % 2
            yk = [("y", k) for k in range(KD)]
            res = []
            pend = []

            def s_pair(j, pr):
                def f():
                    slot = unit_slot[6 + j]
                    sv = slotv(slot)
                    jj = 2 * j + pr
                    bg, bv = next_bank(), next_bank()
                    wg = lambda k: sv[:, k, (2 * pr) * 128:(2 * pr + 1) * 128]
                    wv = lambda k: sv[:, k, (2 * pr + 1) * 128:(2 * pr + 2) * 128]
                    lst = [(psb[bg][:, 0:T], wg(k), y[:, k, 0:T], k == 0, k == KD - 1) for k in range(KD)]
                    lst += [(psb[bv][:, 0:T], wv(k), y[:, k, 0:T], k == 0, k == KD - 1) for k in range(KD)]
                    P.op("pe", _MM(lst), reads=yk + [("ring", slot)], writes=[PS(bg), PS(bv)])
                    if first:
                        bh = next_bank()
                        lst = [(psb[bh][:, 0:2], wg(k), yh[:, k, :], k == 0, k == KD - 1) for k in range(KD)]
                        lst += [(psb[bh][:, 2:4], wv(k), yh[:, k, :], k == 0, k == KD - 1) for k in range(KD)]
                        P.op("pe", _MM(lst), reads=["yh", ("ring", slot)], writes=[PS(bh)])
                        P.op("act", _ACT(out=uh[cur][:, jj, :, :], in_=psb[bh][:, 0:4].rearrange("p (a b) -> p a b", a=2),
                                         func=AF.Identity, bias=0.0, scale=vcol(C_HM)),
                             reads=[PS(bh), "vecs"], writes=[("uh", cur, jj)])
                    ag, av = acc[jj % NSET]
                    bbg = bbuf[jj % 2]
                    AK = ("acc", jj % NSET)
                    BK = ("bb", jj % 2)
                    cw = lambda tap, v: vcol(C_CFW + tap * 44 + v * NFC + jj)
                    cb = lambda v: vcol(C_CFB + v * NFC + jj)
                    P.op("act", _ACT(out=ag[:, 0:T], in_=psb[bg][:, 0:T], func=AF.Identity, bias=cb(0), scale=cw(2, 0)),
                         reads=[PS(bg), "vecs"], writes=[AK + (0,)])
                    P.op("act", _ACT(out=bbg[:, 0:T], in_=psb[bg][:, 0:T], func=AF.Identity, bias=0.0, scale=cw(1, 0)),
                         reads=[PS(bg), "vecs"], writes=[BK])
                    P.op("act", _ACT(out=av[:, 0:T], in_=psb[bv][:, 0:T], func=AF.Identity, bias=cb(1), scale=cw(2, 1)),
                         reads=[PS(bv), "vecs"], writes=[AK + (1,)])
                    for v, (ab, pb) in enumerate(((ag, bg), (av, bv))):
                        K = AK + (v,)
                        after_act = [K, BK] if v == 0 else [K]
                        if not last_tile:
                            P.op("dve", _CP(out=uh[nxt][:, jj, v, :], in_=psb[pb][:, T - 2:T]),
                                 reads=[PS(pb)] + after_act,
                                 writes=[("uh", nxt, jj, v)] + ([("uh", nxt, jj)] if ti == 2 else []))
                        P.op("dve", _STT(out=ab[:, 2:T], in0=psb[pb][:, 0:T - 2], scalar=cw(0, v), in1=ab[:, 2:T],
                                         op0=ALU.mult, op1=ALU.add),
                             reads=[PS(pb), "vecs"] + after_act, writes=[K])
                        if v == 1:
                            P.op("dve", _STT(out=ab[:, 1:T], in0=psb[pb][:, 0:T - 1], scalar=cw(1, v), in1=ab[:, 1:T],
                                             op0=ALU.mult, op1=ALU.add),
                                 reads=[PS(pb), "vecs", K], writes=[K])
                        if first:
                            P.op("dve", _STT(out=ab[:, 0:2], in0=uh[cur][:, jj, v, :], scalar=cw(0, v), in1=ab[:, 0:2],
                                             op0=ALU.mult, op1=ALU.add),
                                 reads=[("uh", cur, jj), "vecs", K], writes=[K])
                            P.op("dve", _STT(out=ab[:, 0:1], in0=uh[cur][:, jj, v, 1:2], scalar=cw(1, v), in1=ab[:, 0:1],
                                             op0=ALU.mult, op1=ALU.add),
                                 reads=[("uh", cur, jj), "vecs", K], writes=[K])
                        else:
                            P.op("dve", _TT(out=ab[:, 0:2], in0=ab[:, 0:2], in1=corr[:, jj, v, :], op=ALU.add),
                                 reads=["corr", K], writes=[K])
                        if v == 0:
                            P.op("pool", _TT(out=ab[:, 1:T], in0=ab[:, 1:T], in1=bbg[:, 0:T - 1], op=ALU.add),
                                 reads=[K, BK], writes=[K])

                    def tail(ag=ag, av=av, jj=jj, AK=AK):
                        P.op("act", _ACT(out=ag[:, 0:T], in_=ag[:, 0:T], func=AF.Silu), reads=[AK + (0,)], writes=[AK + (0,)])
                        P.op("pool", _TT(out=g[:, jj, 0:T], in0=ag[:, 0:T], in1=av[:, 0:T], op=ALU.mult),
                             reads=[AK + (0,), AK + (1,)], writes=[("g", jj)])
                    if pend:
                        pend.pop(0)()
                    pend.append(tail)
                    if jj == NFC - 1:
                        pend.pop(0)()
                return f

            def s_corr():
                w0v = vecs[:, C_CFW:C_CFW + 44].rearrange("p (v j) -> p j v", v=2)
                w1v = vecs[:, C_CFW + 44:C_CFW + 88].rearrange("p (v j) -> p j v", v=2)
                uk = [("uh", cur, jj, v) for jj in range(NFC) for v in range(2)]
                P.op("dve", _TT(out=corr[:, :, :, 0], in0=uh[cur][:, :, :, 0], in1=w0v, op=ALU.mult),
                     reads=uk + ["vecs"], writes=["corr"])
                P.op("dve", _TT(out=ctmp[:], in0=uh[cur][:, :, :, 1], in1=w1v, op=ALU.mult),
                     reads=uk + ["vecs"], writes=["ctmp"])
                P.op("dve", _TT(out=corr[:, :, :, 0], in0=corr[:, :, :, 0], in1=ctmp[:], op=ALU.add),
                     reads=["corr", "ctmp"], writes=["corr"])
                P.op("dve", _TT(out=corr[:, :, :, 1], in0=uh[cur][:, :, :, 1], in1=w0v, op=ALU.mult),
                     reads=uk + ["vecs", "corr"], writes=["corr"])

            def s_unit(j):
                f0, f1 = s_pair(j, 0), s_pair(j, 1)

                def f():
                    if j == 0 and not first:
                        s_corr()
                    f0()
                    f1()
                return f

            for j in range(11):
                res.append((s_unit(j), 6 + j))

            dstate = {}

            def s_down(sw, gi):
                def f():
                    if gi == 0:
                        dstate["banks"] = [next_bank(hold=True) for _ in range(4)]
                    banks = dstate["banks"]
                    f0, n = dn_groups[gi]
                    slot = unit_slot[17 + sw * 3 + gi]
                    rv = ring[slot][:, 0:n * 512].rearrange("p (c e) -> p c e", c=n)
                    lst = []
                    for i in range(4):
                        for c in range(n):
                            fc = f0 + c
                            lst.append((psb[banks[i]][:, 0:T], rv[:, c, i * 128:(i + 1) * 128], g[:, fc, 0:T],
                                        fc == 0, fc == NFC - 1))
                    wr = [PS(b) for b in banks] if gi in (0, 2) else []
                    P.op("pe", _MM(lst), reads=[("g", f0 + c) for c in range(n)] + [("ring", slot)], writes=wr)
                    if gi == 2:
                        for i in range(4):
                            dc = sw * 4 + i
                            P.op("dve", _TT(out=xs[:, dc, 0:T], in0=xs[:, dc, 0:T], in1=psb[banks[i]][:, 0:T], op=ALU.add),
                                 reads=[PS(banks[i]), ("x", s, dc)], writes=[("x", s, dc)])
                        release(*banks)
                return f

            for sw in range(2):
                for gi in range(3):
                    res.append((s_down(sw, gi), 17 + sw * 3 + gi))
            return res

        def store_out(ti, c0, T, ks=tuple(range(KD))):
            def f():
                s = ti % 2
                oc = c0 - HALO
                for k in ks:
                    src, key = (cu[:, k, 0:T], ("cu", k)) if k < 4 else (vt[:, k - 4, 0:T], ("vt", k - 4))
                    P.op("sp", _DMA([(outT_v[:, k, oc:oc + T], src)]), reads=[key], writes=[("out", ti, k)],
                         sem=f"os{s}_{k}", inc=16)
            return f

        tiles_sched = [(0, 0, HALO, True)] + [(1 + i, HALO + i * TT, TT, False) for i in range(NT_)]
        plan = []

        def lx(ti, k0=0, k1=KD):
            t = tiles_sched[ti]
            return (lambda: load_x(t[0], t[1], t[2], k0, k1), None)

        plan.append(lx(0))
        plan.append(lx(1))
        for ti in (0, 1):
            _, c0, T, is_halo = tiles_sched[ti]
            plan += rms_stages(ti, T, C_G1, "y1")
            ms = mixer_stages(ti, T, is_halo)
            if ti == 0:
                ms.insert(1, (build_diag, None))
            plan += ms
            plan += rms_stages(ti, T, C_G2, "y2")
            if is_halo:
                plan.append((lambda T=T: P.op("dve", _CP(out=yh[:], in_=y[:, :, T - 2:T]),
                                              reads=[("y", k) for k in range(KD)], writes=["yh"]), None))
                if NT_ >= 2:
                    plan.append(lx(2))
        SLOT_MAP = _DBG.get("slot_map") or [4, 5, 5,
                                            6, 6, 7, 7, 7, 9,
                                            9, 9, 10, 10,
                                            10, 11, 11, 12, 12,
                                            13, 13, 14]
        for ti in range(1, NT_ + 1):
            _, c0, T, _ = tiles_sched[ti]
            slots = ffn_slots(ti, T, first=(ti == 1), last_tile=(ti == NT_))
            inter = {}
            if ti > 1:
                _, pc0, pT, _ = tiles_sched[ti - 1]
                r3 = rms_stages(ti - 1, pT, C_G3, "final")
                inter.setdefault(0, []).append(r3[0])
                inter.setdefault(1, []).append(r3[1])
                inter.setdefault(1, []).append(r3[2])
                if ti + 1 <= NT_:
                    inter[1].append(lx(ti + 1, 0, 4))
                    inter[1].append(lx(ti + 1, 4, 8))
                for i_, sl_ in enumerate((3, 4, 5, 6)):
                    inter.setdefault(sl_, []).append((store_out(ti - 1, pc0, pT, (i_, 4 + i_)), None))
            if ti + 1 <= NT_:
                _, nc0, nT, _ = tiles_sched[ti + 1]
                stg = rms_stages(ti + 1, nT, C_G1, "y1") + mixer_stages(ti + 1, nT, False) + rms_stages(ti + 1, nT, C_G2, "y2")
                assert len(stg) == len(SLOT_MAP)
                for st_, sl in zip(stg, SLOT_MAP):
                    inter.setdefault(sl, []).append(st_)
            for si, ent in enumerate(slots):
                plan.append(ent)
                plan.extend(inter.get(si, []))
        _, lc0, lT, _ = tiles_sched[NT_]
        plan += rms_stages(NT_, lT, C_G3, "final")
        plan.append((store_out(NT_, lc0, lT), None))

        for fn, u in plan:
            if u is not None:
                pending.append([u, False])
        for fn, u in plan:
            if u is not None:
                need(u)
            if _DBG.get('trace_stage'):
                _DBG['trace_stage'](getattr(fn, '__qualname__', str(fn)))
            fn()
        P.op("sp", None, reads=[("out", ti, k) for ti in range(1, NT_ + 1) for k in range(KD)], inc=0)
        assert not pending
        P.emit(block, sems)
    return nc


def _host_inputs(x, mix_norm_g, w_in, b_in, conv_a_w, conv_a_b, ln_a_g, ln_a_b, ln_b_g, ln_b_b,
                 w_spatial, b_spatial, w_out, ffn_norm_g, w_up, conv_f_w, conv_f_b, w_down, final_norm_g):
    f = lambda v: np.asarray(v, dtype=np.float32)
    x = f(x)
    col = lambda v, n: f(v).reshape(n, 128).T
    vecs = np.zeros((128, NV), np.float32)
    vecs[:, C_G1:C_G1 + 8] = col(mix_norm_g[0], 8)
    vecs[:, C_G2:C_G2 + 8] = col(ffn_norm_g[0], 8)
    vecs[:, C_G3:C_G3 + 8] = col(final_norm_g, 8)
    vecs[:, C_BIN:C_BIN + 12] = col(f(b_in[0])[:1536], 12)
    vecs[:, C_CB:C_CB + 4] = col(conv_a_b[0], 4)
    vecs[:, C_LAG:C_LAG + 4] = col(ln_a_g[0], 4)
    vecs[:, C_LAB:C_LAB + 4] = col(ln_a_b[0], 4)
    cfw = f(conv_f_w[0])
    for tap in range(3):
        vecs[:, C_CFW + tap * 44:C_CFW + (tap + 1) * 44] = col(cfw[tap], 44)
    vecs[:, C_CFB:C_CFB + 44] = col(conv_f_b[0], 44)
    vecs[:, C_ONE] = 1.0
    vecs[:, C_EPSR] = RMS_EPS
    vecs[:, C_EPSL] = LN_EPS
    caw = f(conv_a_w[0])
    for ch in range(4):
        vecs[:, C_CAW + ch * 31:C_CAW + (ch + 1) * 31] = caw[:, ch * 128:(ch + 1) * 128].T
    tiles = np.empty((128, 3, 512), np.float32)
    tiles[:, 0, :] = f(b_in[0])[None, 1536:2048]
    tiles[:, 1, :] = f(ln_b_g[0])[None, :]
    tiles[:, 2, :] = f(ln_b_b[0])[None, :]
    wsT = np.ascontiguousarray(f(w_spatial[0]).transpose(2, 0, 1))
    maskT = np.triu(np.ones((128, 128), np.float32))
    bsr = np.repeat(f(b_spatial[0]), 64, axis=0).reshape(4, 128, 128).transpose(1, 0, 2)
    common = {
        "w_in": np.ascontiguousarray(f(w_in[0])), "w_out": np.ascontiguousarray(f(w_out[0])),
        "w_up": np.ascontiguousarray(f(w_up[0])), "w_down": np.ascontiguousarray(f(w_down[0])),
        "tiles": tiles, "wsT": wsT, "maskT": maskT, "bs": np.ascontiguousarray(bsr),
        "ident": np.eye(128, dtype=np.float32),
    }
    in_maps = []
    for c in range(NCORES):
        b, half = divmod(c, 2)
        t0 = half * TOK
        xt = np.zeros((D, HALO + TOK), np.float32)
        xt[:, HALO:] = x[b, t0:t0 + TOK, :].T
        v = vecs.copy()
        if half:
            xt[:, :HALO] = x[b, t0 - HALO:t0, :].T
            v[:, C_HM] = 1.0
        m = dict(common)
        m["xT"] = xt
        m["vecs"] = v
        in_maps.append(m)
    return in_maps


def kernel(**inputs):
    in_maps = _host_inputs(**inputs)
    nc = build_nc()
    res = run_bass_kernel_spmd(nc, in_maps, core_ids=list(range(NCORES)))
    out = np.empty((4, SEQ, D), np.float32)
    for c in range(NCORES):
        b, half = divmod(c, 2)
        out[b, half * TOK:(half + 1) * TOK, :] = res.results[c]["outT"].T
    return out
```

```python
import numpy as np
from contextlib import ExitStack
import concourse.bass as bass
import concourse.mybir as mybir
from concourse.bass_utils import run_bass_kernel_spmd

F32 = mybir.dt.float32
BF16 = mybir.dt.bfloat16
AF = mybir.ActivationFunctionType
ALU = mybir.AluOpType

NCORES = 8
D = 1024
SEQ = 8192
TOK = 4096
HALO = 128
TT = 512
NT = TOK // TT
DFF = 2816
NFC = DFF // 128
KD = D // 128
RMS_EPS = 1e-6
LN_EPS = 1e-5
NSLOT = 4
NSET = 3
SLOT_EL = 4096
NUNIT = 23

C_G1, C_G2, C_G3 = 0, 8, 16
C_BIN = 24
C_CB = 36
C_LAG = 40
C_LAB = 44
C_CFW = 48
C_CFB = 180
C_HM = 224
C_ONE = 225
C_CAW = 226
C_EPSR = C_CAW + 124
C_EPSL = C_EPSR + 1
NV = C_EPSL + 1


class _OpRec:
    __slots__ = ("seq", "done", "clock")

    def __init__(self, seq, done, clock):
        self.seq = seq
        self.done = done
        self.clock = clock


class _Prog:
    ENG_SEM = {"pe": "s_pe", "act": "s_act", "dve": "s_dve", "pool": "s_pool"}

    def __init__(self):
        self.streams = {e: [] for e in ("pe", "act", "dve", "pool", "sp")}
        self.cnt = {}
        self.clock = {e: {} for e in self.streams}
        self.last_w = {}
        self.readers = {}
        self.seq = 0

    def op(self, eng, fn, reads=(), writes=(), sem=None, inc=1):
        deps = {}
        for k in reads:
            w = self.last_w.get(k)
            if w is not None:
                deps[w.seq] = w
        for k in writes:
            w = self.last_w.get(k)
            if w is not None:
                deps[w.seq] = w
            for r in self.readers.get(k, ()):
                deps[r.seq] = r
        clk = self.clock[eng]
        waits = []
        for s in sorted(deps, reverse=True):
            d = deps[s]
            sn, v = d.done
            if clk.get(sn, 0) >= v:
                continue
            waits.append((sn, v))
            clk[sn] = v
            for a, b in d.clock.items():
                if clk.get(a, 0) < b:
                    clk[a] = b
        semname = sem or self.ENG_SEM.get(eng)
        if inc:
            self.cnt[semname] = self.cnt.get(semname, 0) + inc
            done = (semname, self.cnt[semname])
        else:
            done = ("none", 0)
        self.seq += 1
        rec = _OpRec(self.seq, done, dict(clk))
        self.streams[eng].append((waits, fn, semname, inc))
        for k in writes:
            self.last_w[k] = rec
            self.readers[k] = []
        for k in reads:
            self.readers.setdefault(k, []).append(rec)
        return rec

    def emit(self, block, sems):
        def run(engname):
            def body(eng):
                for waits, fn, semname, inc in self.streams[engname]:
                    for sn, v in waits:
                        eng.wait_ge(sems[sn], v)
                    if fn is None:
                        continue
                    instrs = fn(eng)
                    if not inc:
                        continue
                    if not isinstance(instrs, (list, tuple)):
                        instrs = [instrs]
                    per = inc // len(instrs)
                    assert per * len(instrs) == inc
                    for ins in instrs:
                        ins.then_inc(sems[semname], per)
            return body

        block.tensor(run("pe"))
        block.scalar(run("act"))
        block.vector(run("dve"))
        block.gpsimd(run("pool"))
        block.sync(run("sp"))


def _ACT(**kw):
    return lambda e: e.activation(**kw)


def _TT(**kw):
    return lambda e: e.tensor_tensor(**kw)


def _STT(**kw):
    return lambda e: e.scalar_tensor_tensor(**kw)


def _TS(**kw):
    return lambda e: e.tensor_scalar(**kw)


def _CP(**kw):
    return lambda e: e.tensor_copy(**kw)


def _RCP(**kw):
    return lambda e: e.reciprocal(**kw)


def _MS(ap, val):
    return lambda e: e.memset(ap, val)


def _BNS(**kw):
    return lambda e: e.bn_stats(**kw)


def _BNA(**kw):
    return lambda e: e.bn_aggr(**kw)


def _MM(lst):
    def f(e):
        last = None
        for (o, l, r, st, sp) in lst:
            last = e.matmul(o, l, r, start=st, stop=sp)
        return last
    return f


def _DMA(pairs):
    return lambda e: [e.dma_start(out=o, in_=i) for o, i in pairs]


_DBG = {}


def build_nc(nt=NT):
    TOK = nt * TT
    NT_ = nt
    nc = bass.Bass("TRN2", target_bir_lowering=False)
    dt = nc.dram_tensor
    xT = dt("xT", [D, HALO + TOK], F32, kind="ExternalInput").ap()
    w_in = dt("w_in", [D, 2048], F32, kind="ExternalInput").ap()
    w_out = dt("w_out", [D, D], F32, kind="ExternalInput").ap()
    w_up = dt("w_up", [D, 2 * DFF], F32, kind="ExternalInput").ap()
    w_down = dt("w_down", [DFF, D], F32, kind="ExternalInput").ap()
    vecs_d = dt("vecs", [128, NV], F32, kind="ExternalInput").ap()
    tiles_d = dt("tiles", [128, 3, 512], F32, kind="ExternalInput").ap()
    wsT_d = dt("wsT", [128, 8, 128], F32, kind="ExternalInput").ap()
    mask_d = dt("maskT", [128, 128], F32, kind="ExternalInput").ap()
    bs_d = dt("bs", [128, 4, 128], F32, kind="ExternalInput").ap()
    ident_d = dt("ident", [128, 128], F32, kind="ExternalInput").ap()
    outT = dt("outT", [D, TOK], F32, kind="ExternalOutput").ap()
    scr = dt("scr", [NUNIT, 128, SLOT_EL], BF16, kind="Internal").ap()

    xT_v = xT.rearrange("(k p) t -> p k t", p=128)
    outT_v = outT.rearrange("(k p) t -> p k t", p=128)
    w_in_v = w_in.rearrange("(k p) e -> p k e", p=128)
    w_out_v = w_out.rearrange("(k p) e -> p k e", p=128)
    w_up_v = w_up.rearrange("(k p) e -> p k e", p=128)
    w_dn_v = w_down.rearrange("(c p) e -> p c e", p=128)

    P = _Prog()
    semnames = ["s_pe", "s_act", "s_dve", "s_pool", "cst"]
    semnames += [f"xl{i}_{k}" for i in range(2) for k in (0, 4, 6)] + [f"os{i}_{k}" for i in range(2) for k in range(KD)]
    semnames += [f"rl{i}" for i in range(NSLOT)] + [f"rs{i}" for i in range(NSLOT)] + [f"rc{i}" for i in range(NSLOT)]

    with ExitStack() as es:
        sems = {n: es.enter_context(nc.semaphore(n)) for n in semnames}
        sb = lambda name, shape, dtp: es.enter_context(nc.sbuf_tensor(name, shape, dtp))
        ring = [sb(f"ring{i}", [128, SLOT_EL], BF16) for i in range(NSLOT)]
        xbuf = [sb(f"xb{i}", [128, KD, TT], F32) for i in range(2)]
        y = sb("y", [128, KD, TT], BF16)
        y1 = sb("y1", [128, KD, TT], BF16)
        mx = sb("mx", [128, KD, TT], BF16)
        a = sb("a", [128, 4, 30 + TT], BF16)
        cu = sb("cu", [128, 4, TT], F32)
        vt = sb("vt", [128, 4, 512], F32)
        vlnz = sb("vlnz", [128, 4, 4, 2, 128], BF16)
        wsT = vt[:, 0:2, :].rearrange("p a (b c) -> p (a b) c", c=128)
        st_r = sb("st_r", [128, TT], F32)
        st_mu = sb("st_mu", [128, TT], F32)
        st_t = sb("st_t", [128, TT], F32)
        st_ra = sb("st_ra", [128, TT], F32)
        g = sb("g", [128, NFC, TT], BF16)
        acc = [[sb(f"acc{i}{j}", [128, TT], F32) for j in range(2)] for i in range(NSET)]
        uh = [sb(f"uh{i}", [128, NFC, 2, 2], F32) for i in range(2)]
        bbuf = [sb(f"bb{i}", [128, TT], F32) for i in range(2)]
        corr = sb("corr", [128, NFC, 2, 2], F32)
        ctmp = sb("ctmp", [128, NFC, 2], F32)
        yh = sb("yh", [128, KD, 2], BF16)
        diag = sb("diag", [128, 124, 128], BF16)
        maskT = sb("mask_s", [128, 128], F32)
        wsm = sb("wsm", [128, 8, 128], BF16)
        bs = sb("bs_s", [128, 4, 128], F32)
        tiles = sb("tiles_s", [128, 3, 512], F32)
        ident = sb("ident_s", [128, 128], F32)
        vecs = sb("vecs_s", [128, NV], F32)
        onesm = sb("onesm", [128, 128], BF16)
        ones5 = sb("ones5", [128, 128], BF16)
        bst = sb("bst", [128, 4, 6], F32)
        bag = sb("bag", [128, 4, 2], F32)
        rsb = sb("rsb", [128, 4], F32)
        psb = [es.enter_context(nc.psum_tensor(f"ps{i}", [128, 512], F32)) for i in range(8)]
        block = es.enter_context(nc.Block())

        state = {"bank": 0, "unit_seq": 0, "held": set()}

        def next_bank(hold=False):
            for _ in range(8):
                b = state["bank"]
                state["bank"] = (b + 1) % 8
                if b not in state["held"]:
                    if hold:
                        state["held"].add(b)
                    return b
            raise RuntimeError("all PSUM banks held")

        def release(*banks):
            for b in banks:
                state["held"].discard(b)

        PS = lambda b: ("ps", b)
        vcol = lambda c: vecs[:, c:c + 1]

        P.op("sp", _DMA([(vecs[:], vecs_d), (tiles[:], tiles_d), (wsT, wsT_d), (maskT[:], mask_d),
                         (bs[:], bs_d), (ident[:], ident_d)]),
             writes=["vecs", "tiles", ("vt", 0), ("vt", 1), "maskT", "bs", "ident"], sem="cst", inc=16 * 6)
        P.op("pool", _MS(vlnz[:], 0.0), writes=["vlnz"])
        P.op("pool", _MS(a[:], 0.0), writes=[("a", c) for c in range(4)])
        P.op("dve", _MS(onesm[:], 1.0 / D), writes=["onesm"])
        P.op("dve", _MS(ones5[:], 1.0 / 512), writes=["ones5"])
        for h in range(8):
            P.op("dve", _TT(out=wsm[:, h, :], in0=wsT[:, h, :], in1=maskT[:], op=ALU.mult),
                 reads=[("vt", 0), ("vt", 1), "maskT"], writes=[("wsm", h)])

        def build_diag():
            for i in range(124):
                if i % 2 == 0:
                    P.op("dve", _TS(out=diag[:, i, :], in0=ident[:], scalar1=vcol(C_CAW + i), scalar2=None, op0=ALU.mult),
                         reads=["ident", "vecs"], writes=[("diag", i)])
                else:
                    P.op("act", _ACT(out=diag[:, i, :], in_=ident[:], func=AF.Identity, bias=0.0, scale=vcol(C_CAW + i)),
                         reads=["ident", "vecs"], writes=[("diag", i)])

        in_cols = {0: 512, 1: 0, 2: 1024, 3: 1536}
        dn_groups = [(0, 8), (8, 8), (16, 6)]
        unit_in_scr = set()
        unit_slot = {}
        pending = []

        def unit_src_dmas(u, slot):
            r = ring[slot]
            rv = r[:].rearrange("p (k e) -> p k e", k=KD)
            if u < 4:
                c0 = in_cols[u]
                return [(rv, w_in_v[:, :, c0:c0 + 512])]
            if u < 6:
                c0 = (u - 4) * 512
                return [(rv, w_out_v[:, :, c0:c0 + 512])]
            if u < 17:
                j = u - 6
                res = []
                for pr in range(2):
                    jj = 2 * j + pr
                    res.append((rv[:, :, (2 * pr) * 128:(2 * pr + 1) * 128], w_up_v[:, :, jj * 128:(jj + 1) * 128]))
                    res.append((rv[:, :, (2 * pr + 1) * 128:(2 * pr + 2) * 128],
                                w_up_v[:, :, DFF + jj * 128:DFF + (jj + 1) * 128]))
                return res
            q = u - 17
            sw, gi = q // 3, q % 3
            f0, n = dn_groups[gi]
            return [(r[:, 0:n * 512].rearrange("p (c e) -> p c e", c=n), w_dn_v[:, f0:f0 + n, sw * 512:(sw + 1) * 512])]

        def issue_load(u):
            slot = state["unit_seq"] % NSLOT
            state["unit_seq"] += 1
            unit_slot[u] = slot
            if _DBG.get('noscr') or u not in unit_in_scr:
                pairs = unit_src_dmas(u, slot)
                P.op("pool", _DMA(pairs), writes=[("ring", slot)], sem=f"rc{slot}", inc=16 * len(pairs))
                if not _DBG.get('noscr'):
                    P.op("sp", _DMA([(scr[u], ring[slot][:])]), reads=[("ring", slot)], writes=[("scr", u)],
                         sem=f"rs{slot}", inc=16)
                unit_in_scr.add(u)
            else:
                P.op("sp", _DMA([(ring[slot][:], scr[u])]), reads=[("scr", u)], writes=[("ring", slot)],
                     sem=f"rl{slot}", inc=16)

        def prefetch():
            for ent in pending[:NSLOT - 1]:
                if not ent[1]:
                    issue_load(ent[0])
                    ent[1] = True

        def need(u):
            assert pending and pending[0][0] == u, (u, pending[:3])
            prefetch()
            pending.pop(0)
            prefetch()

        def load_x(ti, c0, T, k0=0, k1=KD):
            s = ti % 2
            P.op("sp", _DMA([(xbuf[s][:, k0:k1, 0:T], xT_v[:, k0:k1, c0:c0 + T])]),
                 writes=[("x", s, k) for k in range(k0, k1)], sem=f"xl{s}_{k0}", inc=16)

        def slotv(slot):
            return ring[slot][:].rearrange("p (k e) -> p k e", k=KD)

        def rms_stages(ti, T, gcol, mode):
            s = ti % 2
            xs = xbuf[s]
            st = {}

            def s_sq():
                for k in range(KD):
                    P.op("act", _ACT(out=mx[:, k, 0:T], in_=xs[:, k, 0:T], func=AF.Square),
                         reads=[("x", s, k)], writes=[("mx", k)])

            def s_stat():
                b = next_bank(hold=True)
                st["b"] = b
                P.op("pe", _MM([(psb[b][:, 0:T], onesm[:], mx[:, k, 0:T], k == 0, k == KD - 1) for k in range(KD)]),
                     reads=[("mx", k) for k in range(KD)] + ["onesm"], writes=[PS(b)])

            def s_y():
                b = st["b"]
                P.op("act", _ACT(out=st_r[:, 0:T], in_=psb[b][:, 0:T], func=AF.Ln, bias=vcol(C_EPSR), scale=1.0),
                     reads=[PS(b), "vecs"], writes=["st_r"])
                release(b)
                P.op("act", _ACT(out=st_r[:, 0:T], in_=st_r[:, 0:T], func=AF.Exp, scale=-0.5), reads=["st_r"], writes=["st_r"])
                for k in range(KD):
                    if mode == "final":
                        if k < 4:
                            dst, key = cu[:, k, 0:T], ("cu", k)
                        else:
                            dst, key = vt[:, k - 4, 0:T], ("vt", k - 4)
                    elif mode == "y1":
                        dst, key = y1[:, k, 0:T], ("y1", k)
                    else:
                        dst, key = y[:, k, 0:T], ("y", k)
                    P.op("dve", _STT(out=dst, in0=xs[:, k, 0:T], scalar=vcol(gcol + k), in1=st_r[:, 0:T],
                                     op0=ALU.mult, op1=ALU.mult),
                         reads=[("x", s, k), "st_r", "vecs"], writes=[key])
            return [(s_sq, None), (s_stat, None), (s_y, None)]

        def mixer_stages(ti, T, is_halo):
            s = ti % 2
            xs = xbuf[s]
            nsub = T // 128
            yk = [("y1", k) for k in range(KD)]
            st = {}

            def s_in(u):
                def f():
                    slot = unit_slot[u]
                    sv = slotv(slot)
                    for ecl in range(4):
                        b = next_bank()
                        P.op("pe", _MM([(psb[b][:, 0:T], sv[:, k, ecl * 128:(ecl + 1) * 128], y1[:, k, 0:T], k == 0, k == KD - 1)
                                        for k in range(KD)]),
                             reads=yk + [("ring", slot)], writes=[PS(b)])
                        if u == 0:
                            P.op("act", _ACT(out=cu[:, ecl, 0:T], in_=psb[b][:, 0:T], func=AF.Sigmoid,
                                             bias=vcol(C_BIN + 4 + ecl), scale=1.0),
                                 reads=[PS(b), "vecs"], writes=[("cu", ecl)])
                        elif u == 1:
                            P.op("dve", _STT(out=a[:, ecl, 30:30 + T], in0=psb[b][:, 0:T], scalar=vcol(C_BIN + ecl),
                                             in1=cu[:, ecl, 0:T], op0=ALU.add, op1=ALU.mult),
                                 reads=[PS(b), "vecs", ("cu", ecl)], writes=[("a", ecl)])
                        else:
                            P.op("act", _ACT(out=cu[:, ecl, 0:T], in_=psb[b][:, 0:T], func=AF.Gelu,
                                             bias=vcol(C_BIN + 8 + ecl), scale=1.0),
                                 reads=[PS(b), "vecs"], writes=[("cu", ecl)])
                return f

            def s_invv():
                slot = unit_slot[3]
                sv = slotv(slot)
                for sidx in range(nsub):
                    b = next_bank()
                    P.op("pe", _MM([(psb[b][:, 0:512], y1[:, k, sidx * 128:(sidx + 1) * 128], sv[:, k, 0:512], k == 0, k == KD - 1)
                                    for k in range(KD)]),
                         reads=yk + [("ring", slot)], writes=[PS(b)])
                    P.op("dve", _TT(out=vt[:, sidx, :], in0=psb[b][:, 0:512], in1=tiles[:, 0, :], op=ALU.add),
                         reads=[PS(b), "tiles"], writes=[("vt", sidx)])
                    P.op("act", _ACT(out=vt[:, sidx, :], in_=vt[:, sidx, :], func=AF.Gelu),
                         reads=[("vt", sidx)], writes=[("vt", sidx)])
                    P.op("dve", _BNS(out=bst[:, sidx, :], in_=vt[:, sidx, :]), reads=[("vt", sidx)], writes=[("bst", sidx)])
                    P.op("dve", _BNA(out=bag[:, sidx, :], in_=bst[:, sidx, :]), reads=[("bst", sidx)], writes=[("bag", sidx)])

            def s_conv(ch):
                def f():
                    b = next_bank()
                    P.op("pe", _MM([(psb[b][:, 0:T], diag[:, ch * 31 + tap, :], a[:, ch, tap:tap + T], tap == 0, tap == 30)
                                    for tap in range(31)]),
                         reads=[("a", ch)] + [("diag", ch * 31 + tap) for tap in range(31)], writes=[PS(b)])
                    P.op("act", _ACT(out=cu[:, ch, 0:T], in_=psb[b][:, 0:T], func=AF.Identity, bias=vcol(C_CB + ch), scale=1.0),
                         reads=[PS(b), "vecs"], writes=[("cu", ch)])
                    P.op("act", _ACT(out=y1[:, ch, 0:T], in_=psb[b][:, 0:T], func=AF.Identity, bias=vcol(C_CB + ch), scale=1.0),
                         reads=[PS(b), "vecs"], writes=[("y1", ch)])
                    P.op("act", _ACT(out=y1[:, 4 + ch, 0:T], in_=psb[b][:, 0:T], func=AF.Square, bias=vcol(C_CB + ch), scale=1.0),
                         reads=[PS(b), "vecs"], writes=[("y1", 4 + ch)])
                return f

            def s_vln():
                P.op("act", _ACT(out=rsb[:, 0:nsub], in_=bag[:, 0:nsub, 1], func=AF.Ln, bias=vcol(C_EPSL), scale=1.0),
                     reads=[("bag", i) for i in range(nsub)] + ["vecs"], writes=["rsb"])
                P.op("act", _ACT(out=rsb[:, 0:nsub], in_=rsb[:, 0:nsub], func=AF.Exp, scale=-0.5), reads=["rsb"], writes=["rsb"])
                for sidx in range(nsub):
                    P.op("dve", _TS(out=vt[:, sidx, :], in0=vt[:, sidx, :], scalar1=bag[:, sidx, 0:1],
                                    scalar2=rsb[:, sidx:sidx + 1], op0=ALU.subtract, op1=ALU.mult),
                         reads=[("vt", sidx), ("bag", sidx), "rsb"], writes=[("vt", sidx)])
                    P.op("dve", _TT(out=vt[:, sidx, :], in0=vt[:, sidx, :], in1=tiles[:, 1, :], op=ALU.mult),
                         reads=[("vt", sidx), "tiles"], writes=[("vt", sidx)])
                    for hh in range(2):
                        P.op("dve", _TT(out=vlnz[:, sidx, :, hh, hh * 64:(hh + 1) * 64],
                                        in0=vt[:, sidx, :].rearrange("p (c q) -> p c q", c=4)[:, :, hh * 64:(hh + 1) * 64],
                                        in1=tiles[:, 2, :].rearrange("p (c q) -> p c q", c=4)[:, :, hh * 64:(hh + 1) * 64],
                                        op=ALU.add),
                             reads=[("vt", sidx), "tiles", "vlnz"], writes=[("vlnz", sidx, hh)])

            def s_sp():
                for ch in range(4):
                    b = next_bank()
                    lst = []
                    for sidx in range(nsub):
                        lst.append((psb[b][:, sidx * 128:(sidx + 1) * 128], vlnz[:, sidx, ch, 0, :], wsm[:, 2 * ch, :], True, False))
                        lst.append((psb[b][:, sidx * 128:(sidx + 1) * 128], vlnz[:, sidx, ch, 1, :], wsm[:, 2 * ch + 1, :], False, True))
                    P.op("pe", _MM(lst),
                         reads=[("vlnz", i, hh) for i in range(nsub) for hh in range(2)] + [("wsm", 2 * ch), ("wsm", 2 * ch + 1)],
                         writes=[PS(b)])
                    for sidx in range(nsub):
                        P.op("dve", _TT(out=st_t[:, sidx * 128:(sidx + 1) * 128], in0=psb[b][:, sidx * 128:(sidx + 1) * 128],
                                        in1=bs[:, ch, :], op=ALU.add),
                             reads=[PS(b), "bs"], writes=["st_t"])
                    P.op("dve", _TT(out=mx[:, 4 + ch, 0:T], in0=st_t[:, 0:T], in1=cu[:, ch, 0:T], op=ALU.mult),
                         reads=["st_t", ("cu", ch)], writes=[("mx", 4 + ch)])

            def s_cev():
                mcol = C_HM if is_halo else C_ONE
                P.op("dve", _TS(out=a[:, :, 0:30], in0=a[:, :, T:T + 30], scalar1=vcol(mcol), scalar2=None, op0=ALU.mult),
                     reads=[("a", c) for c in range(4)] + ["vecs"], writes=[("a", c) for c in range(4)])

            def s_st2():
                bm = next_bank(hold=True)
                be = next_bank(hold=True)
                st["bm"], st["be"] = bm, be
                lst = [(psb[bm][:, 0:T], ones5[:], y1[:, ch, 0:T], ch == 0, ch == 3) for ch in range(4)]
                lst += [(psb[be][:, 0:T], ones5[:], y1[:, 4 + ch, 0:T], ch == 0, ch == 3) for ch in range(4)]
                P.op("pe", _MM(lst), reads=yk + ["ones5"], writes=[PS(bm), PS(be)])

            def s_lna():
                bm, be = st["bm"], st["be"]
                P.op("dve", _CP(out=st_mu[:, 0:T], in_=psb[bm][:, 0:T]), reads=[PS(bm)], writes=["st_mu"])
                P.op("dve", _TT(out=st_t[:, 0:T], in0=st_mu[:, 0:T], in1=st_mu[:, 0:T], op=ALU.mult),
                     reads=["st_mu"], writes=["st_t"])
                P.op("dve", _TT(out=st_t[:, 0:T], in0=psb[be][:, 0:T], in1=st_t[:, 0:T], op=ALU.subtract),
                     reads=[PS(be), "st_t"], writes=["st_t"])
                release(bm, be)
                P.op("act", _ACT(out=st_ra[:, 0:T], in_=st_t[:, 0:T], func=AF.Ln, bias=vcol(C_EPSL), scale=1.0),
                     reads=["st_t", "vecs"], writes=["st_ra"])
                P.op("act", _ACT(out=st_ra[:, 0:T], in_=st_ra[:, 0:T], func=AF.Exp, scale=-0.5), reads=["st_ra"], writes=["st_ra"])
                for ch in range(4):
                    P.op("dve", _TT(out=cu[:, ch, 0:T], in0=cu[:, ch, 0:T], in1=st_mu[:, 0:T], op=ALU.subtract),
                         reads=[("cu", ch), "st_mu"], writes=[("cu", ch)])
                    P.op("dve", _TT(out=cu[:, ch, 0:T], in0=cu[:, ch, 0:T], in1=st_ra[:, 0:T], op=ALU.mult),
                         reads=[("cu", ch), "st_ra"], writes=[("cu", ch)])
                    P.op("act", _ACT(out=mx[:, ch, 0:T], in_=cu[:, ch, 0:T], func=AF.Silu,
                                     bias=vcol(C_LAB + ch), scale=vcol(C_LAG + ch)),
                         reads=[("cu", ch), "vecs"], writes=[("mx", ch)])

            def s_out(u):
                def f():
                    mxk = [("mx", k) for k in range(KD)]
                    slot = unit_slot[u]
                    sv = slotv(slot)
                    for dcl in range(4):
                        dc = (u - 4) * 4 + dcl
                        b = next_bank()
                        P.op("pe", _MM([(psb[b][:, 0:T], sv[:, k, dcl * 128:(dcl + 1) * 128], mx[:, k, 0:T], k == 0, k == KD - 1)
                                        for k in range(KD)]),
                             reads=mxk + [("ring", slot)], writes=[PS(b)])
                        P.op("dve", _TT(out=xs[:, dc, 0:T], in0=xs[:, dc, 0:T], in1=psb[b][:, 0:T], op=ALU.add),
                             reads=[PS(b), ("x", s, dc)], writes=[("x", s, dc)])
                return f

            return [(s_in(0), 0), (s_in(1), 1), (s_in(2), 2), (s_invv, 3), (s_vln, None), (s_sp, None),
                    (s_conv(0), None), (s_conv(1), None), (s_conv(2), None), (s_conv(3), None),
                    (s_cev, None), (s_st2, None), (s_lna, None), (s_out(4), 4), (s_out(5), 5)]

        def ffn_slots(ti, T, first, last_tile):
            s = ti % 2
            xs = xbuf[s]
            cur, nxt = ti % 2, (ti + 1) % 2
            yk = [("y", k) for k in range(KD)]
            res = []
            pend = []

            def s_pair(j, pr):
                def f():
                    slot = unit_slot[6 + j]
                    sv = slotv(slot)
                    jj = 2 * j + pr
                    bg, bv = next_bank(), next_bank()
                    wg = lambda k: sv[:, k, (2 * pr) * 128:(2 * pr + 1) * 128]
                    wv = lambda k: sv[:, k, (2 * pr + 1) * 128:(2 * pr + 2) * 128]
                    lst = [(psb[bg][:, 0:T], wg(k), y[:, k, 0:T], k == 0, k == KD - 1) for k in range(KD)]
                    lst += [(psb[bv][:, 0:T], wv(k), y[:, k, 0:T], k == 0, k == KD - 1) for k in range(KD)]
                    P.op("pe", _MM(lst), reads=yk + [("ring", slot)], writes=[PS(bg), PS(bv)])
                    if first:
                        bh = next_bank()
                        lst = [(psb[bh][:, 0:2], wg(k), yh[:, k, :], k == 0, k == KD - 1) for k in range(KD)]
                        lst += [(psb[bh][:, 2:4], wv(k), yh[:, k, :], k == 0, k == KD - 1) for k in range(KD)]
                        P.op("pe", _MM(lst), reads=["yh", ("ring", slot)], writes=[PS(bh)])
                        P.op("act", _ACT(out=uh[cur][:, jj, :, :], in_=psb[bh][:, 0:4].rearrange("p (a b) -> p a b", a=2),
                                         func=AF.Identity, bias=0.0, scale=vcol(C_HM)),
                             reads=[PS(bh), "vecs"], writes=[("uh", cur, jj)])
                    ag, av = acc[jj % NSET]
                    bbg = bbuf[jj % 2]
                    AK = ("acc", jj % NSET)
                    BK = ("bb", jj % 2)
                    cw = lambda tap, v: vcol(C_CFW + tap * 44 + v * NFC + jj)
                    cb = lambda v: vcol(C_CFB + v * NFC + jj)
                    P.op("act", _ACT(out=ag[:, 0:T], in_=psb[bg][:, 0:T], func=AF.Identity, bias=cb(0), scale=cw(2, 0)),
                         reads=[PS(bg), "vecs"], writes=[AK + (0,)])
                    P.op("act", _ACT(out=bbg[:, 0:T], in_=psb[bg][:, 0:T], func=AF.Identity, bias=0.0, scale=cw(1, 0)),
                         reads=[PS(bg), "vecs"], writes=[BK])
                    P.op("act", _ACT(out=av[:, 0:T], in_=psb[bv][:, 0:T], func=AF.Identity, bias=cb(1), scale=cw(2, 1)),
                         reads=[PS(bv), "vecs"], writes=[AK + (1,)])
                    for v, (ab, pb) in enumerate(((ag, bg), (av, bv))):
                        K = AK + (v,)
                        after_act = [K, BK] if v == 0 else [K]
                        if not last_tile:
                            P.op("dve", _CP(out=uh[nxt][:, jj, v, :], in_=psb[pb][:, T - 2:T]),
                                 reads=[PS(pb)] + after_act,
                                 writes=[("uh", nxt, jj, v)] + ([("uh", nxt, jj)] if ti == 2 else []))
                        P.op("dve", _STT(out=ab[:, 2:T], in0=psb[pb][:, 0:T - 2], scalar=cw(0, v), in1=ab[:, 2:T],
                                         op0=ALU.mult, op1=ALU.add),
                             reads=[PS(pb), "vecs"] + after_act, writes=[K])
                        if v == 1:
                            P.op("dve", _STT(out=ab[:, 1:T], in0=psb[pb][:, 0:T - 1], scalar=cw(1, v), in1=ab[:, 1:T],
                                             op0=ALU.mult, op1=ALU.add),
                                 reads=[PS(pb), "vecs", K], writes=[K])
                        if first:
                            P.op("dve", _STT(out=ab[:, 0:2], in0=uh[cur][:, jj, v, :], scalar=cw(0, v), in1=ab[:, 0:2],
                                             op0=ALU.mult, op1=ALU.add),
                                 reads=[("uh", cur, jj), "vecs", K], writes=[K])
                            P.op("dve", _STT(out=ab[:, 0:1], in0=uh[cur][:, jj, v, 1:2], scalar=cw(1, v), in1=ab[:, 0:1],
                                             op0=ALU.mult, op1=ALU.add),
                                 reads=[("uh", cur, jj), "vecs", K], writes=[K])
                        else:
                            P.op("dve", _TT(out=ab[:, 0:2], in0=ab[:, 0:2], in1=corr[:, jj, v, :], op=ALU.add),
                                 reads=["corr", K], writes=[K])
                        if v == 0:
                            P.op("pool", _TT(out=ab[:, 1:T], in0=ab[:, 1:T], in1=bbg[:, 0:T - 1], op=ALU.add),
                                 reads=[K, BK], writes=[K])

                    def tail(ag=ag, av=av, jj=jj, AK=AK):
                        P.op("act", _ACT(out=ag[:, 0:T], in_=ag[:, 0:T], func=AF.Silu), reads=[AK + (0,)], writes=[AK + (0,)])
                        P.op("pool", _TT(out=g[:, jj, 0:T], in0=ag[:, 0:T], in1=av[:, 0:T], op=ALU.mult),
                             reads=[AK + (0,), AK + (1,)], writes=[("g", jj)])
                    if pend:
                        pend.pop(0)()
                    pend.append(tail)
                    if jj == NFC - 1:
                        pend.pop(0)()
                return f

            def s_corr():
                w0v = vecs[:, C_CFW:C_CFW + 44].rearrange("p (v j) -> p j v", v=2)
                w1v = vecs[:, C_CFW + 44:C_CFW + 88].rearrange("p (v j) -> p j v", v=2)
                uk = [("uh", cur, jj, v) for jj in range(NFC) for v in range(2)]
                P.op("dve", _TT(out=corr[:, :, :, 0], in0=uh[cur][:, :, :, 0], in1=w0v, op=ALU.mult),
                     reads=uk + ["vecs"], writes=["corr"])
                P.op("dve", _TT(out=ctmp[:], in0=uh[cur][:, :, :, 1], in1=w1v, op=ALU.mult),
                     reads=uk + ["vecs"], writes=["ctmp"])
                P.op("dve", _TT(out=corr[:, :, :, 0], in0=corr[:, :, :, 0], in1=ctmp[:], op=ALU.add),
                     reads=["corr", "ctmp"], writes=["corr"])
                P.op("dve", _TT(out=corr[:, :, :, 1], in0=uh[cur][:, :, :, 1], in1=w0v, op=ALU.mult),
                     reads=uk + ["vecs", "corr"], writes=["corr"])

            def s_unit(j):
                f0, f1 = s_pair(j, 0), s_pair(j, 1)

                def f():
                    if j == 0 and not first:
                        s_corr()
                    f0()
                    f1()
                return f

            for j in range(11):
                res.append((s_unit(j), 6 + j))

            dstate = {}

            def s_down(sw, gi):
                def f():
                    if gi == 0:
                        dstate["banks"] = [next_bank(hold=True) for _ in range(4)]
                    banks = dstate["banks"]
                    f0, n = dn_groups[gi]
                    slot = unit_slot[17 + sw * 3 + gi]
                    rv = ring[slot][:, 0:n * 512].rearrange("p (c e) -> p c e", c=n)
                    lst = []
                    for i in range(4):
                        for c in range(n):
                            fc = f0 + c
                            lst.append((psb[banks[i]][:, 0:T], rv[:, c, i * 128:(i + 1) * 128], g[:, fc, 0:T],
                                        fc == 0, fc == NFC - 1))
                    wr = [PS(b) for b in banks] if gi in (0, 2) else []
                    P.op("pe", _MM(lst), reads=[("g", f0 + c) for c in range(n)] + [("ring", slot)], writes=wr)
                    if gi == 2:
                        for i in range(4):
                            dc = sw * 4 + i
                            P.op("dve", _TT(out=xs[:, dc, 0:T], in0=xs[:, dc, 0:T], in1=psb[banks[i]][:, 0:T], op=ALU.add),
                                 reads=[PS(banks[i]), ("x", s, dc)], writes=[("x", s, dc)])
                        release(*banks)
                return f

            for sw in range(2):
                for gi in range(3):
                    res.append((s_down(sw, gi), 17 + sw * 3 + gi))
            return res

        def store_out(ti, c0, T, ks=tuple(range(KD))):
            def f():
                s = ti % 2
                oc = c0 - HALO
                for k in ks:
                    src, key = (cu[:, k, 0:T], ("cu", k)) if k < 4 else (vt[:, k - 4, 0:T], ("vt", k - 4))
                    P.op("sp", _DMA([(outT_v[:, k, oc:oc + T], src)]), reads=[key], writes=[("out", ti, k)],
                         sem=f"os{s}_{k}", inc=16)
            return f

        tiles_sched = [(0, 0, HALO, True)] + [(1 + i, HALO + i * TT, TT, False) for i in range(NT_)]
        plan = []

        def lx(ti, k0=0, k1=KD):
            t = tiles_sched[ti]
            return (lambda: load_x(t[0], t[1], t[2], k0, k1), None)

        plan.append(lx(0))
        plan.append(lx(1))
        for ti in (0, 1):
            _, c0, T, is_halo = tiles_sched[ti]
            plan += rms_stages(ti, T, C_G1, "y1")
            ms = mixer_stages(ti, T, is_halo)
            if ti == 0:
                ms.insert(1, (build_diag, None))
            plan += ms
            plan += rms_stages(ti, T, C_G2, "y2")
            if is_halo:
                plan.append((lambda T=T: P.op("dve", _CP(out=yh[:], in_=y[:, :, T - 2:T]),
                                              reads=[("y", k) for k in range(KD)], writes=["yh"]), None))
                if NT_ >= 2:
                    plan.append(lx(2))
        SLOT_MAP = _DBG.get("slot_map") or [4, 5, 5,
                                            6, 6, 7, 7, 7, 9,
                                            9, 9, 10, 10,
                                            10, 11, 11, 13, 13,
                                            14, 14, 15]
        for ti in range(1, NT_ + 1):
            _, c0, T, _ = tiles_sched[ti]
            slots = ffn_slots(ti, T, first=(ti == 1), last_tile=(ti == NT_))
            inter = {}
            if ti > 1:
                _, pc0, pT, _ = tiles_sched[ti - 1]
                r3 = rms_stages(ti - 1, pT, C_G3, "final")
                inter.setdefault(0, []).append(r3[0])
                inter.setdefault(1, []).append(r3[1])
                inter.setdefault(1, []).append(r3[2])
                if ti + 1 <= NT_:
                    inter[1].append(lx(ti + 1, 0, 4))
                    inter[1].append(lx(ti + 1, 4, 8))
                for i_, sl_ in enumerate((3, 4, 5, 6)):
                    inter.setdefault(sl_, []).append((store_out(ti - 1, pc0, pT, (i_, 4 + i_)), None))
            if ti + 1 <= NT_:
                _, nc0, nT, _ = tiles_sched[ti + 1]
                stg = rms_stages(ti + 1, nT, C_G1, "y1") + mixer_stages(ti + 1, nT, False) + rms_stages(ti + 1, nT, C_G2, "y2")
                assert len(stg) == len(SLOT_MAP)
                for st_, sl in zip(stg, SLOT_MAP):
                    inter.setdefault(sl, []).append(st_)
            for si, ent in enumerate(slots):
                plan.append(ent)
                plan.extend(inter.get(si, []))
        _, lc0, lT, _ = tiles_sched[NT_]
        plan += rms_stages(NT_, lT, C_G3, "final")
        plan.append((store_out(NT_, lc0, lT), None))

        for fn, u in plan:
            if u is not None:
                pending.append([u, False])
        for fn, u in plan:
            if u is not None:
                need(u)
            if _DBG.get('trace_stage'):
                _DBG['trace_stage'](getattr(fn, '__qualname__', str(fn)))
            fn()
        P.op("sp", None, reads=[("out", ti, k) for ti in range(1, NT_ + 1) for k in range(KD)], inc=0)
        assert not pending
        P.emit(block, sems)
    return nc


def _host_inputs(x, mix_norm_g, w_in, b_in, conv_a_w, conv_a_b, ln_a_g, ln_a_b, ln_b_g, ln_b_b,
                 w_spatial, b_spatial, w_out, ffn_norm_g, w_up, conv_f_w, conv_f_b, w_down, final_norm_g):
    f = lambda v: np.asarray(v, dtype=np.float32)
    x = f(x)
    col = lambda v, n: f(v).reshape(n, 128).T
    vecs = np.zeros((128, NV), np.float32)
    vecs[:, C_G1:C_G1 + 8] = col(mix_norm_g[0], 8)
    vecs[:, C_G2:C_G2 + 8] = col(ffn_norm_g[0], 8)
    vecs[:, C_G3:C_G3 + 8] = col(final_norm_g, 8)
    vecs[:, C_BIN:C_BIN + 12] = col(f(b_in[0])[:1536], 12)
    vecs[:, C_CB:C_CB + 4] = col(conv_a_b[0], 4)
    vecs[:, C_LAG:C_LAG + 4] = col(ln_a_g[0], 4)
    vecs[:, C_LAB:C_LAB + 4] = col(ln_a_b[0], 4)
    cfw = f(conv_f_w[0])
    for tap in range(3):
        vecs[:, C_CFW + tap * 44:C_CFW + (tap + 1) * 44] = col(cfw[tap], 44)
    vecs[:, C_CFB:C_CFB + 44] = col(conv_f_b[0], 44)
    vecs[:, C_ONE] = 1.0
    vecs[:, C_EPSR] = RMS_EPS
    vecs[:, C_EPSL] = LN_EPS
    caw = f(conv_a_w[0])
    for ch in range(4):
        vecs[:, C_CAW + ch * 31:C_CAW + (ch + 1) * 31] = caw[:, ch * 128:(ch + 1) * 128].T
    tiles = np.empty((128, 3, 512), np.float32)
    tiles[:, 0, :] = f(b_in[0])[None, 1536:2048]
    tiles[:, 1, :] = f(ln_b_g[0])[None, :]
    tiles[:, 2, :] = f(ln_b_b[0])[None, :]
    wsT = np.ascontiguousarray(f(w_spatial[0]).transpose(2, 0, 1))
    maskT = np.triu(np.ones((128, 128), np.float32))
    bsr = np.repeat(f(b_spatial[0]), 64, axis=0).reshape(4, 128, 128).transpose(1, 0, 2)
    common = {
        "w_in": np.ascontiguousarray(f(w_in[0])), "w_out": np.ascontiguousarray(f(w_out[0])),
        "w_up": np.ascontiguousarray(f(w_up[0])), "w_down": np.ascontiguousarray(f(w_down[0])),
        "tiles": tiles, "wsT": wsT, "maskT": maskT, "bs": np.ascontiguousarray(bsr),
        "ident": np.eye(128, dtype=np.float32),
    }
    in_maps = []
    for c in range(NCORES):
        b, half = divmod(c, 2)
        t0 = half * TOK
        xt = np.zeros((D, HALO + TOK), np.float32)
        xt[:, HALO:] = x[b, t0:t0 + TOK, :].T
        v = vecs.copy()
        if half:
            xt[:, :HALO] = x[b, t0 - HALO:t0, :].T
            v[:, C_HM] = 1.0
        m = dict(common)
        m["xT"] = xt
        m["vecs"] = v
        in_maps.append(m)
    return in_maps


def kernel(**inputs):
    in_maps = _host_inputs(**inputs)
    nc = build_nc()
    res = run_bass_kernel_spmd(nc, in_maps, core_ids=list(range(NCORES)))
    out = np.empty((4, SEQ, D), np.float32)
    for c in range(NCORES):
        b, half = divmod(c, 2)
        out[b, half * TOK:(half + 1) * TOK, :] = res.results[c]["outT"].T
    return out
```
